# Optimizing a Trainium2 kernel written in Bass

```python
import jax, jax.numpy as jnp
from jax import lax
import numpy as np


D_MODEL = 2048
BATCH = 4
SEQ = 2048
DEPTH = 2
DEC_BATCH = 128
DEC_SEQ = 1
PAST_LEN = 16384
PAGE_SIZE = 128

N_MIXERS = 2
N_RG_LAYERS = (DEPTH + 1) // 2
N_GLA_LAYERS = DEPTH // 2
D_FF = 5632
RMS_EPS = 1e-6
D_RNN = D_MODEL
RG_BLOCKS = 8
RG_BLOCK_W = D_RNN // RG_BLOCKS
CONV_W = 4
RG_C = 8.0
GLA_HEADS = 4
GLA_DK_TOTAL = D_MODEL // 2
GLA_DV_TOTAL = D_MODEL
GLA_DK = GLA_DK_TOTAL // GLA_HEADS
GLA_DV = GLA_DV_TOTAL // GLA_HEADS
GLA_GATE_RANK = 16
GLA_GATE_NORM = 16.0
GLA_CHUNK = 64

kernel_name = "hybrid_rglru_gla_macaron_step"


def rmsnorm(x, g):
    xf = x.astype(jnp.float32)
    y = xf * lax.rsqrt(jnp.mean(xf * xf, axis=-1, keepdims=True) + RMS_EPS)
    return (y * g.astype(jnp.float32)).astype(x.dtype)


def swiglu(x, w1, w3, w2):
    return (jax.nn.silu(x @ w1) * (x @ w3)) @ w2


def causal_conv(xb, buf, w, bias):
    T = xb.shape[1]
    xp = jnp.concatenate([buf.astype(xb.dtype), xb], axis=1)
    out = bias
    for j in range(CONV_W):
        out = out + xp[:, j:j + T] * w[j]
    new_buf = xp[:, xp.shape[1] - (CONV_W - 1):]
    return out, new_buf


def block_diag(x, w):
    B, T, _ = x.shape
    xr = x.reshape(B, T, RG_BLOCKS, RG_BLOCK_W)
    return jnp.einsum('btnc,ncd->btnd', xr, w).reshape(B, T, D_RNN)


def rglru(xc, h0, w_a, b_a, w_i, b_i, lam):
    xf = xc.astype(jnp.float32)
    r = jax.nn.sigmoid(block_diag(xc, w_a).astype(jnp.float32) + b_a.astype(jnp.float32))
    i = jax.nn.sigmoid(block_diag(xc, w_i).astype(jnp.float32) + b_i.astype(jnp.float32))
    log_a = -RG_C * r * jax.nn.softplus(-lam.astype(jnp.float32))
    a = jnp.exp(log_a)
    b = jnp.sqrt(-jnp.expm1(2.0 * log_a)) * (i * xf)
    b = b.at[:, 0].add(a[:, 0] * h0.astype(jnp.float32))

    def combine(p, q):
        return (p[0] * q[0], q[0] * p[1] + q[1])

    _, h = lax.associative_scan(combine, (a, b), axis=1)
    return h, h[:, -1]


def rg_block(x, conv_buf, h0, w_y, w_x, conv_w, conv_b, w_a, b_a, w_i, b_i, lam, w_o):
    gate = jax.nn.gelu(x @ w_y, approximate=True)
    xc, new_buf = causal_conv(x @ w_x, conv_buf, conv_w, conv_b)
    h, h_last = rglru(xc, h0, w_a, b_a, w_i, b_i, lam)
    out = (h.astype(x.dtype) * gate) @ w_o
    return out, h_last, new_buf


def _pad_t(a, pad):
    return jnp.pad(a, ((0, 0), (0, pad), (0, 0), (0, 0)))


def gla_chunked(q, k, v, g, S0):
    B, T, H, _ = q.shape
    C = min(GLA_CHUNK, T)
    pad = (-T) % C
    q, k, v, g = (_pad_t(t, pad) for t in (q, k, v, g))
    n = (T + pad) // C

    def to_chunks(a):
        return a.reshape(B, n, C, H, a.shape[-1]).transpose(1, 0, 3, 2, 4)

    qc, kc, vc, gc = (to_chunks(t) for t in (q, k, v, g))
    mask = jnp.tril(jnp.ones((C, C), dtype=bool))

    def step(S, inp):
        qb, kb, vb, gb = inp
        bcum = jnp.cumsum(gb, axis=-2)
        b_last = bcum[..., -1:, :]
        q_i = qb * jnp.exp(bcum)
        k_i = kb * jnp.exp(-bcum)
        k_e = kb * jnp.exp(b_last - bcum)
        scores = jnp.where(mask, jnp.einsum('bhtk,bhsk->bhts', q_i, k_i), 0.0)
        o = jnp.einsum('bhtk,bhkv->bhtv', q_i, S) + jnp.einsum('bhts,bhsv->bhtv', scores, vb)
        S_new = jnp.exp(b_last[..., 0, :])[..., None] * S + jnp.einsum('bhsk,bhsv->bhkv', k_e, vb)
        return S_new, o

    S_fin, o = lax.scan(step, S0, (qc, kc, vc, gc))
    o = o.transpose(1, 0, 3, 2, 4).reshape(B, n * C, H, GLA_DV)[:, :T]
    return o, S_fin


def gla_block(x, S0, w_q, w_k, w_v, w_g1, w_g2, b_g, w_r, onorm, w_o):
    B, T, _ = x.shape
    f32 = jnp.float32
    q = (x @ w_q).astype(f32).reshape(B, T, GLA_HEADS, GLA_DK) * (GLA_DK ** -0.5)
    k = (x @ w_k).astype(f32).reshape(B, T, GLA_HEADS, GLA_DK)
    v = (x @ w_v).astype(f32).reshape(B, T, GLA_HEADS, GLA_DV)
    g_logit = ((x @ w_g1) @ w_g2 + b_g).astype(f32)
    g = (jax.nn.log_sigmoid(g_logit) / GLA_GATE_NORM).reshape(B, T, GLA_HEADS, GLA_DK)
    o, S_fin = gla_chunked(q, k, v, g, S0.astype(f32))
    o = rmsnorm(o, onorm).reshape(B, T, GLA_DV_TOTAL).astype(x.dtype)
    out = (o * jax.nn.silu(x @ w_r)) @ w_o
    return out, S_fin


def setup_inputs(seed: int = 0) -> dict:
    key = jax.random.key(seed)
    ks = jax.random.split(key, 48)
    kit = iter([ks[i] for i in range(48)])
    f32 = jnp.float32

    def w(shape, fan_in):
        return jax.random.normal(next(kit), shape, f32) * fan_in ** -0.5

    def gain(shape):
        return 1.0 + 0.05 * jax.random.normal(next(kit), shape, f32)

    def bias(shape):
        return 0.02 * jax.random.normal(next(kit), shape, f32)

    d = {}
    d['x_prompt'] = jax.random.normal(next(kit), (BATCH, SEQ, D_MODEL), f32)
    d['x_sample'] = jax.random.normal(next(kit), (DEC_BATCH, DEC_SEQ, D_MODEL), f32)
    d['state_rglru_h'] = jax.random.normal(next(kit), (N_RG_LAYERS, DEC_BATCH, D_RNN), f32)
    d['state_rglru_conv'] = jax.random.normal(next(kit), (N_RG_LAYERS, DEC_BATCH, CONV_W - 1, D_RNN), f32)
    d['state_gla_S'] = jax.random.normal(next(kit), (N_GLA_LAYERS, DEC_BATCH, GLA_HEADS, GLA_DK, GLA_DV), f32)
    d['ln_ffn1'] = gain((DEPTH, D_MODEL))
    d['ffn1_w1'] = w((DEPTH, D_MODEL, D_FF), D_MODEL)
    d['ffn1_w3'] = w((DEPTH, D_MODEL, D_FF), D_MODEL)
    d['ffn1_w2'] = w((DEPTH, D_FF, D_MODEL), D_FF)
    d['ln_mix'] = gain((DEPTH, D_MODEL))
    d['ln_ffn2'] = gain((DEPTH, D_MODEL))
    d['ffn2_w1'] = w((DEPTH, D_MODEL, D_FF), D_MODEL)
    d['ffn2_w3'] = w((DEPTH, D_MODEL, D_FF), D_MODEL)
    d['ffn2_w2'] = w((DEPTH, D_FF, D_MODEL), D_FF)
    d['rg_w_y'] = w((N_RG_LAYERS, D_MODEL, D_RNN), D_MODEL)
    d['rg_w_x'] = w((N_RG_LAYERS, D_MODEL, D_RNN), D_MODEL)
    d['rg_conv_w'] = w((N_RG_LAYERS, CONV_W, D_RNN), CONV_W)
    d['rg_conv_b'] = bias((N_RG_LAYERS, D_RNN))
    d['rg_w_a'] = w((N_RG_LAYERS, RG_BLOCKS, RG_BLOCK_W, RG_BLOCK_W), RG_BLOCK_W)
    d['rg_b_a'] = bias((N_RG_LAYERS, D_RNN))
    d['rg_w_i'] = w((N_RG_LAYERS, RG_BLOCKS, RG_BLOCK_W, RG_BLOCK_W), RG_BLOCK_W)
    d['rg_b_i'] = bias((N_RG_LAYERS, D_RNN))
    a0 = jax.random.uniform(next(kit), (N_RG_LAYERS, D_RNN), f32, minval=0.9, maxval=0.999)
    d['rg_lambda'] = jnp.log(a0) - jnp.log1p(-a0)
    d['rg_w_o'] = w((N_RG_LAYERS, D_RNN, D_MODEL), D_RNN)
    d['gla_w_q'] = w((N_GLA_LAYERS, D_MODEL, GLA_DK_TOTAL), D_MODEL)
    d['gla_w_k'] = w((N_GLA_LAYERS, D_MODEL, GLA_DK_TOTAL), D_MODEL)
    d['gla_w_v'] = w((N_GLA_LAYERS, D_MODEL, GLA_DV_TOTAL), D_MODEL)
    d['gla_w_g1'] = w((N_GLA_LAYERS, D_MODEL, GLA_GATE_RANK), D_MODEL)
    d['gla_w_g2'] = w((N_GLA_LAYERS, GLA_GATE_RANK, GLA_DK_TOTAL), GLA_GATE_RANK)
    d['gla_b_g'] = bias((N_GLA_LAYERS, GLA_DK_TOTAL))
    d['gla_w_r'] = w((N_GLA_LAYERS, D_MODEL, GLA_DV_TOTAL), D_MODEL)
    d['gla_onorm'] = gain((N_GLA_LAYERS, GLA_DV))
    d['gla_w_o'] = w((N_GLA_LAYERS, GLA_DV_TOTAL, D_MODEL), GLA_DV_TOTAL)
    d['ln_final'] = gain((D_MODEL,))
    return d


def reference(x_prompt, x_sample, state_rglru_h, state_rglru_conv, state_gla_S,
              ln_ffn1, ffn1_w1, ffn1_w3, ffn1_w2, ln_mix, ln_ffn2, ffn2_w1, ffn2_w3, ffn2_w2,
              rg_w_y, rg_w_x, rg_conv_w, rg_conv_b, rg_w_a, rg_b_a, rg_w_i, rg_b_i, rg_lambda, rg_w_o,
              gla_w_q, gla_w_k, gla_w_v, gla_w_g1, gla_w_g2, gla_b_g, gla_w_r, gla_onorm, gla_w_o,
              ln_final):

    def run(x, rg_h, rg_conv, gla_S):
        new_h, new_conv, new_S = [], [], []
        for i in range(DEPTH):
            x = x + 0.5 * swiglu(rmsnorm(x, ln_ffn1[i]), ffn1_w1[i], ffn1_w3[i], ffn1_w2[i])
            u = rmsnorm(x, ln_mix[i])
            j = i // N_MIXERS
            if i % N_MIXERS == 0:
                m, h_last, c_new = rg_block(u, rg_conv[j], rg_h[j], rg_w_y[j], rg_w_x[j], rg_conv_w[j],
                                            rg_conv_b[j], rg_w_a[j], rg_b_a[j], rg_w_i[j], rg_b_i[j],
                                            rg_lambda[j], rg_w_o[j])
                new_h.append(h_last.astype(rg_h.dtype))
                new_conv.append(c_new.astype(rg_conv.dtype))
            else:
                m, S_fin = gla_block(u, gla_S[j], gla_w_q[j], gla_w_k[j], gla_w_v[j], gla_w_g1[j],
                                     gla_w_g2[j], gla_b_g[j], gla_w_r[j], gla_onorm[j], gla_w_o[j])
                new_S.append(S_fin.astype(gla_S.dtype))
            x = x + m
            x = x + 0.5 * swiglu(rmsnorm(x, ln_ffn2[i]), ffn2_w1[i], ffn2_w3[i], ffn2_w2[i])
        return rmsnorm(x, ln_final), jnp.stack(new_h), jnp.stack(new_conv), jnp.stack(new_S)

    Bp = x_prompt.shape[0]
    dt = x_prompt.dtype
    h0_p = jnp.zeros((N_RG_LAYERS, Bp, D_RNN), dt)
    conv0_p = jnp.zeros((N_RG_LAYERS, Bp, CONV_W - 1, D_RNN), dt)
    S0_p = jnp.zeros((N_GLA_LAYERS, Bp, GLA_HEADS, GLA_DK, GLA_DV), dt)

    y_prompt, p_h, p_conv, p_S = run(x_prompt, h0_p, conv0_p, S0_p)
    y_sample, s_h, s_conv, s_S = run(x_sample, state_rglru_h, state_rglru_conv, state_gla_S)
    return (y_prompt, y_sample, p_h, p_conv, p_S, s_h, s_conv, s_S)
```

```python
import numpy as np
from contextlib import ExitStack
import concourse.bass as bass
import concourse.mybir as mybir
from concourse.bass_utils import run_bass_kernel_spmd

F32 = mybir.dt.float32
BF16 = mybir.dt.bfloat16
ALU = mybir.AluOpType
AF = mybir.ActivationFunctionType

D = 2048
DC = 16
T = 1024
NSMP = 16
NHALO = 3
NS = NSMP + NHALO
NT = T + NS
FF = 5632
NG = FF // 256
CG = [(0, 512), (512, 1024), (1024, NT)]
EPS = 1e-6
PAIRS = [[0, 1], [2, 3], [4, 5], [6, 7]]


class Res:
    __slots__ = ("name", "w", "r")

    def __init__(self, name):
        self.name = name
        self.w = None
        self.r = []


class Op:
    __slots__ = ("eng", "fn", "deps", "idx", "signal", "kind", "sem", "semval", "prev")


class Prog:
    ENGS = ("pe", "act", "dve", "pool", "sp")
    NDMASEM = 12

    def __init__(self):
        self.ops = []

    def op(self, eng, fn, reads=(), writes=(), kind="c"):
        o = Op()
        o.eng = eng
        o.fn = fn
        o.kind = kind
        o.idx = len(self.ops)
        o.signal = False
        o.sem = None
        o.semval = 0
        o.prev = None
        deps = set()
        for r in reads:
            if r.w is not None:
                deps.add(r.w)
        for w in writes:
            if w.w is not None:
                deps.add(w.w)
            deps.update(w.r)
        for r in reads:
            r.r.append(o.idx)
        for w in writes:
            w.w = o.idx
            w.r = []
        deps.discard(o.idx)
        if eng == "pe" and kind == "c":
            deps = {d for d in deps if not (self.ops[d].eng == "pe" and self.ops[d].kind == "c")}
        o.deps = deps
        self.ops.append(o)
        return o

    def emit(self, nc, es):
        ops = self.ops
        for o in ops:
            for d in o.deps:
                ops[d].signal = True
        LIM = 1000
        esem = {e: [] for e in self.ENGS}
        dsem = {e: [es.enter_context(nc.semaphore("d_%s%d" % (e, i))) for i in range(self.NDMASEM)]
                for e in ("sp", "pool")}
        cnt = {e: 0 for e in self.ENGS}
        dcnt = {e: 0 for e in dsem}
        last_on_sem = {}
        ncc = 0
        for o in ops:
            if o.kind == "dma":
                k = dcnt[o.eng]
                dcnt[o.eng] += 1
                o.sem = dsem[o.eng][k % self.NDMASEM]
                o.semval = 16 * (k // self.NDMASEM + 1)
                o.prev = last_on_sem.get((o.eng, k % self.NDMASEM))
                last_on_sem[(o.eng, k % self.NDMASEM)] = o
            elif o.kind == "cc":
                o.sem = es.enter_context(nc.semaphore("cc%d" % ncc))
                ncc += 1
                o.semval = 1
            elif o.signal:
                cnt[o.eng] += 1
                ep = (cnt[o.eng] - 1) // LIM
                if ep >= len(esem[o.eng]):
                    esem[o.eng].append(es.enter_context(nc.semaphore("s_%s%d" % (o.eng, ep))))
                o.sem = esem[o.eng][ep]
                o.semval = (cnt[o.eng] - 1) % LIM + 1
        self.counts = dict(cnt)
        self.dcounts = dict(dcnt)
        per_eng = {e: [o for o in ops if o.eng == e] for e in self.ENGS}

        def run(eng_name, e):
            waited = {}
            for o in per_eng[eng_name]:
                need = {}
                xs = [ops[d] for d in o.deps]
                if o.kind == "dma" and o.prev is not None:
                    xs.append(o.prev)
                for x in xs:
                    key = id(x.sem)
                    if key not in need or need[key][1] < x.semval:
                        need[key] = (x.sem, x.semval)
                for key, (s, v) in need.items():
                    if waited.get(key, 0) >= v:
                        continue
                    e.wait_ge(s, v)
                    waited[key] = v
                ins = o.fn(e)
                if o.kind == "dma":
                    ins.then_inc(o.sem, 16)
                elif o.kind == "cc":
                    ins.then_inc(o.sem, 1)
                elif o.signal:
                    ins.then_inc(o.sem, 1)

        with nc.Block() as block:
            @block.tensor
            def _(e):
                run("pe", e)

            @block.scalar
            def _(e):
                run("act", e)

            @block.vector
            def _(e):
                run("dve", e)

            @block.gpsimd
            def _(e):
                run("pool", e)

            @block.sync
            def _(e):
                run("sp", e)


WNAMES = ["ln_ffn1", "ffn1_w1", "ffn1_w3", "ffn1_w2", "ln_mix", "ln_ffn2", "ffn2_w1", "ffn2_w3", "ffn2_w2",
          "rg_w_y", "rg_w_x", "rg_conv_w", "rg_conv_b", "rg_w_a", "rg_b_a", "rg_w_i", "rg_b_i", "rg_lambda", "rg_w_o",
          "gla_w_q", "gla_w_k", "gla_w_v", "gla_w_g1", "gla_w_g2", "gla_b_g", "gla_w_r", "gla_onorm", "gla_w_o",
          "ln_final"]
WSHAPES = {
    "ln_ffn1": [2, D], "ffn1_w1": [2, D, FF], "ffn1_w3": [2, D, FF], "ffn1_w2": [2, FF, D], "ln_mix": [2, D],
    "ln_ffn2": [2, D], "ffn2_w1": [2, D, FF], "ffn2_w3": [2, D, FF], "ffn2_w2": [2, FF, D],
    "rg_w_y": [1, D, D], "rg_w_x": [1, D, D], "rg_conv_w": [1, 4, D], "rg_conv_b": [1, D], "rg_w_a": [1, 8, 256, 256],
    "rg_b_a": [1, D], "rg_w_i": [1, 8, 256, 256], "rg_b_i": [1, D], "rg_lambda": [1, D], "rg_w_o": [1, D, D],
    "gla_w_q": [1, D, 1024], "gla_w_k": [1, D, 1024], "gla_w_v": [1, D, D], "gla_w_g1": [1, D, 16],
    "gla_w_g2": [1, 16, 1024], "gla_b_g": [1, 1024], "gla_w_r": [1, D, D], "gla_onorm": [1, 512], "gla_w_o": [1, D, D],
    "ln_final": [D],
}


def build_nc():
    nc = bass.Bass("TRN2", target_bir_lowering=False)
    din = lambda n, s: nc.dram_tensor(n, s, F32, kind="ExternalInput").ap()
    dout = lambda n, s: nc.dram_tensor(n, s, F32, kind="ExternalOutput").ap()
    xp = din("xp", [T, D])
    xsm = din("xsm", [NS, D])
    st_h = din("st_h", [NSMP, D])
    st_conv = din("st_conv", [NSMP * 3, D])
    st_S = din("st_S", [NSMP, 4, 256, 512])
    flag_d = din("flag", [128, 1])
    W = {n: din(n, WSHAPES[n]) for n in WNAMES}
    y_p = dout("y_p", [T, D])
    y_s = dout("y_s", [NSMP, D])
    p_h = dout("p_h", [1, D])
    p_conv = dout("p_conv", [3, D])
    p_S = dout("p_S", [4, 256, 512])
    s_h = dout("s_h", [NSMP, D])
    s_conv = dout("s_conv", [NSMP, 3, D])
    s_S = dout("s_S", [NSMP, 4, 256, 512])
    cc_h_in = [nc.dram_tensor("cc_h_in%d" % i, [128, 2], F32) for i in range(8)]
    cc_h_out = [nc.dram_tensor("cc_h_out%d" % i, [256, 2], F32) for i in range(8)]
    cc_S_in = [nc.dram_tensor("cc_S_in%d" % i, [128, 1024], F32) for i in range(4)]
    cc_S_out = [nc.dram_tensor("cc_S_out%d" % i, [256, 1024], F32) for i in range(4)]

    P = Prog()
    es = ExitStack()
    SB0 = 16512
    off = [SB0]

    def sb(name, shape, dt, at=None):
        nbytes = int(np.prod(shape[1:])) * (4 if dt == F32 else 2)
        nbytes = (nbytes + 31) // 32 * 32
        if at is None:
            o = off[0]
            off[0] += nbytes
        else:
            o = at
        assert o + nbytes <= SB0 + 212800, (name, o, nbytes)
        return nc.alloc_sbuf_tensor_at(name, list(shape), dt, offset=o)

    xT = sb("xT", [128, DC, NT], F32)
    uT = sb("uT", [128, DC, NT], BF16)
    ident = sb("ident", [128, 128], F32)
    ones = sb("ones", [128, 128], F32)
    maskT = sb("maskT", [128, 128], F32)
    gains = sb("gains", [128, 7, DC], F32)
    flag = sb("flag_s", [128, 1], F32)
    scr = sb("scr", [128, 8], F32)
    OV = off[0]
    psb = [nc.alloc_psum_tensor("ps%d" % i, [128, 512], F32) for i in range(8)]
    R_ps = [Res("ps%d" % i) for i in range(8)]
    R_x = [Res("xT%d" % k) for k in range(DC)]
    R_u = [Res("uT%d" % k) for k in range(DC)]
    R_rstd = Res("rstd")
    R_c = Res("consts")
    R_out = Res("outs")

    R_outs = []

    def dma(eng, out, in_, reads, writes, **kw):
        writes = list(writes)
        if R_out in writes:
            r = Res("out%d" % len(R_outs))
            R_outs.append(r)
            writes = [w for w in writes if w is not R_out] + [r]
        P.op(eng, lambda e: e.dma_start(out=out, in_=in_, **kw), reads, writes, kind="dma")

    P.op("pool", lambda e: e.memset(ident[:], 1.0), [], [R_c])
    P.op("pool", lambda e: e.affine_select(out=ident[:], in_=ident[:], pattern=[[-1, 128]], compare_op=ALU.is_equal,
                                           fill=0.0, base=0, channel_multiplier=1), [R_c], [R_c])
    P.op("pool", lambda e: e.memset(maskT[:], 1.0), [R_c], [R_c])
    P.op("pool", lambda e: e.affine_select(out=maskT[:], in_=maskT[:], pattern=[[1, 128]], compare_op=ALU.is_ge,
                                           fill=0.0, base=0, channel_multiplier=-1), [R_c], [R_c])
    P.op("pool", lambda e: e.memset(ones[:], 1.0), [R_c], [R_c])
    gsrc = [W["ln_ffn1"][0], W["ln_mix"][0], W["ln_ffn2"][0], W["ln_ffn1"][1], W["ln_mix"][1], W["ln_ffn2"][1],
            W["ln_final"]]
    R_g = Res("gains")
    for i, g in enumerate(gsrc):
        dma("sp", gains[:, i, :], g.rearrange("(k p) -> p k", p=128), [], [R_g], allow_slow_non_contiguous=True)
    dma("sp", flag[:], flag_d, [], [R_g])

    def tm_to_fm(src_dram, nrows, dst_fn, stage, R_stage, ps_i, writes):
        dma("sp", stage[0:nrows, :], src_dram, [], [R_stage])
        for q in range(4):
            def tr(e, q=q):
                ins = None
                for j in range(4):
                    k = q * 4 + j
                    ins = e.transpose(psb[ps_i][:, j * 128:j * 128 + nrows], stage[0:nrows, k * 128:(k + 1) * 128],
                                      ident[0:nrows, 0:nrows])
                return ins
            P.op("pe", tr, [R_stage, R_c], [R_ps[ps_i]])
            def ev(e, q=q):
                ins = None
                for j in range(4):
                    ins = e.copy(out=dst_fn(q * 4 + j), in_=psb[ps_i][:, j * 128:j * 128 + nrows])
                return ins
            P.op("act", ev, [R_ps[ps_i]], writes(q))

    def fm_to_tm(src_fn, nrows, dst_dram, stage, R_stage, ps_i, reads):
        for q in range(4):
            def tr(e, q=q):
                ins = None
                for j in range(4):
                    ins = e.transpose(psb[ps_i][0:nrows, j * 128:(j + 1) * 128], src_fn(q * 4 + j), ident[:, :])
                return ins
            P.op("pe", tr, list(reads(q)) + [R_c], [R_ps[ps_i]])
            P.op("act", lambda e, q=q: e.copy(out=stage[0:nrows, q * 512:(q + 1) * 512], in_=psb[ps_i][0:nrows, :]),
                 [R_ps[ps_i]], [R_stage])
        dma("sp", dst_dram, stage[0:nrows, :], [R_stage], [R_out])

    off[0] = OV
    wA = [sb("wA%d" % i, [128, DC, 256], BF16) for i in range(3)]
    wB = [sb("wB%d" % i, [128, DC, 256], BF16) for i in range(3)]
    wC = [sb("wC%d" % i, [128, 2, D], BF16) for i in range(3)]
    hT = [sb("hT%d" % i, [128, 2, NT], BF16) for i in range(2)]
    sil = [sb("sil%d" % i, [128, 512], F32) for i in range(2)]
    sq = [sb("sq%d" % i, [128, 512], F32) for i in range(4)]
    rstd = sb("rstd", [128, NT], F32)
    OV2 = off[0]
    stage = [sb("stage%d" % i, [128, D], F32, at=OV + i * 8192) for i in range(2)]
    R_wA = [Res("wA%d" % i) for i in range(3)]
    R_wB = [Res("wB%d" % i) for i in range(3)]
    R_wC = [Res("wC%d" % i) for i in range(3)]
    R_hT = [Res("hT%d" % i) for i in range(2)]
    R_sil = [Res("sil%d" % i) for i in range(2)]
    R_sq = [Res("sq%d" % i) for i in range(4)]
    R_stage = R_wA[:2]
    cnt = {"w": 0, "sq": 0, "pab": 0, "dn": 0}

    for tt in range(T // 128):
        st = tt % 2
        tm_to_fm(xp[tt * 128:(tt + 1) * 128, :], 128, lambda k, tt=tt: xT[:, k, tt * 128:(tt + 1) * 128],
                 stage[st], R_stage[st], st, lambda q: [R_x[q * 4 + j] for j in range(4)])
    tm_to_fm(xsm, NS, lambda k: xT[:, k, T:NT], stage[0], R_stage[0], 0, lambda q: [R_x[q * 4 + j] for j in range(4)])

    def norm(gi, out_fp32=None, after_group=None):
        for ci, (c0, c1) in enumerate(CG):
            n = c1 - c0
            pi = ci % 2
            for k in range(DC):
                s = cnt["sq"] % 4
                cnt["sq"] += 1
                P.op("act", lambda e, k=k, s=s, c0=c0, c1=c1, n=n: e.activation(out=sq[s][:, 0:n], in_=xT[:, k, c0:c1], func=AF.Square),
                     [R_x[k]], [R_sq[s]])
                P.op("pe", lambda e, k=k, s=s, n=n, pi=pi: e.matmul(psb[pi][:, 0:n], lhsT=ones[:, :], rhs=sq[s][:, 0:n],
                                                                    start=(k == 0), stop=(k == DC - 1)),
                     [R_sq[s], R_c], [R_ps[pi]])
            P.op("act", lambda e, c0=c0, c1=c1, n=n, pi=pi: e.activation(out=rstd[:, c0:c1], in_=psb[pi][:, 0:n], func=AF.Sqrt,
                                                                         scale=1.0 / D, bias=EPS),
                 [R_ps[pi]], [R_rstd])
            P.op("dve", lambda e, c0=c0, c1=c1: e.reciprocal(out=rstd[:, c0:c1], in_=rstd[:, c0:c1]), [R_rstd], [R_rstd])
            for k in range(DC):
                if out_fp32 is None:
                    P.op("dve", lambda e, k=k, c0=c0, c1=c1: e.scalar_tensor_tensor(
                        out=uT[:, k, c0:c1], in0=xT[:, k, c0:c1], scalar=gains[:, gi, k:k + 1], in1=rstd[:, c0:c1],
                        op0=ALU.mult, op1=ALU.mult), [R_x[k], R_rstd, R_g], [R_u[k]])
                else:
                    P.op("dve", lambda e, k=k, c0=c0, c1=c1: e.scalar_tensor_tensor(
                        out=xT[:, k, c0:c1], in0=xT[:, k, c0:c1], scalar=gains[:, gi, k:k + 1], in1=rstd[:, c0:c1],
                        op0=ALU.mult, op1=ALU.mult), [R_x[k], R_rstd, R_g], [R_x[k]])
            if after_group is not None:
                after_group(ci)

    def wload(dst, R_dst, src_ap):
        dma("pool", dst, src_ap, [], [R_dst])

    def ffn(l, w1, w3, w2, gi):
        norm(gi)
        w1l, w3l, w2l = w1[l], w3[l], w2[l]

        def load(g):
            s = g % 3
            wload(wA[s][:], R_wA[s], w1l[:, g * 256:(g + 1) * 256].rearrange("(k p) n -> p k n", p=128))
            wload(wB[s][:], R_wB[s], w3l[:, g * 256:(g + 1) * 256].rearrange("(k p) n -> p k n", p=128))
            wload(wC[s][:], R_wC[s], w2l[g * 256:(g + 1) * 256, :].rearrange("(c p) n -> p c n", p=128))

        def up(g):
            s = g % 3
            hb = g % 2
            for fl in range(2):
                for ci, (c0, c1) in enumerate(CG):
                    n = c1 - c0
                    if ci < 2:
                        i = cnt["pab"] % 2
                        cnt["pab"] += 1
                        pa, pb, Ra, Rb = psb[i][:, 0:n], psb[2 + i][:, 0:n], R_ps[i], R_ps[2 + i]
                        so, Rs = sil[i][:, 0:n], R_sil[i]
                    else:
                        pa, pb, Ra, Rb = psb[6][:, 0:n], psb[6][:, 32:32 + n], R_ps[6], R_ps[6]
                        so, Rs = sil[0][:, 0:n], R_sil[0]

                    def mm(e, s=s, fl=fl, c0=c0, c1=c1, pa=pa, pb=pb):
                        ins = None
                        for k in range(DC):
                            e.matmul(pa, lhsT=wA[s][:, k, fl * 128:(fl + 1) * 128], rhs=uT[:, k, c0:c1],
                                     start=(k == 0), stop=(k == DC - 1))
                        for k in range(DC):
                            ins = e.matmul(pb, lhsT=wB[s][:, k, fl * 128:(fl + 1) * 128], rhs=uT[:, k, c0:c1],
                                           start=(k == 0), stop=(k == DC - 1))
                        return ins
                    P.op("pe", mm, [R_wA[s], R_wB[s]] + R_u, list({Ra, Rb}))
                    P.op("act", lambda e, so=so, pa=pa: e.activation(out=so, in_=pa, func=AF.Silu), [Ra], [Rs])
                    P.op("dve", lambda e, hb=hb, fl=fl, c0=c0, c1=c1, so=so, pb=pb: e.tensor_tensor(
                        out=hT[hb][:, fl, c0:c1], in0=so, in1=pb, op=ALU.mult), [Rs, Rb], [R_hT[hb]])

        def down(g):
            s = g % 3
            hb = g % 2
            for j in range(DC):
                for ci, (c0, c1) in enumerate(CG[:2]):
                    di = cnt["dn"] % 6
                    cnt["dn"] += 1
                    bi_ = (4, 5, 0, 1, 2, 3)[di]
                    i = di % 2
                    py = psb[bi_]

                    def mm(e, j=j, c0=c0, c1=c1, py=py):
                        e.matmul(py[:, :], lhsT=wC[s][:, 0, j * 128:(j + 1) * 128], rhs=hT[hb][:, 0, c0:c1], start=True, stop=False)
                        return e.matmul(py[:, :], lhsT=wC[s][:, 1, j * 128:(j + 1) * 128], rhs=hT[hb][:, 1, c0:c1], start=False, stop=True)
                    P.op("pe", mm, [R_wC[s], R_hT[hb]], [R_ps[bi_]])
                    if i == 0:
                        P.op("dve", lambda e, j=j, c0=c0, c1=c1, py=py: e.scalar_tensor_tensor(
                            out=xT[:, j, c0:c1], in0=py[:, :], scalar=0.5, in1=xT[:, j, c0:c1], op0=ALU.mult, op1=ALU.add),
                            [R_ps[bi_], R_x[j]], [R_x[j]])
                    else:
                        q = cnt["sq"] % 4
                        cnt["sq"] += 1
                        P.op("act", lambda e, py=py, q=q: e.activation(out=sq[q][:, :], in_=py[:, :], func=AF.Copy, scale=0.5),
                             [R_ps[bi_]], [R_sq[q]])
                        P.op("pool", lambda e, j=j, c0=c0, c1=c1, q=q: e.tensor_tensor(
                            out=xT[:, j, c0:c1], in0=sq[q][:, :], in1=xT[:, j, c0:c1], op=ALU.add), [R_sq[q], R_x[j]], [R_x[j]])

            def mms(e):
                ins = None
                for j in range(DC):
                    e.matmul(psb[7][:, j * 32:j * 32 + NS], lhsT=wC[s][:, 0, j * 128:(j + 1) * 128], rhs=hT[hb][:, 0, T:NT], start=True, stop=False)
                    ins = e.matmul(psb[7][:, j * 32:j * 32 + NS], lhsT=wC[s][:, 1, j * 128:(j + 1) * 128], rhs=hT[hb][:, 1, T:NT], start=False, stop=True)
                return ins
            P.op("pe", mms, [R_wC[s], R_hT[hb]], [R_ps[7]])
            P.op("dve", lambda e: e.scalar_tensor_tensor(
                out=xT[:, :, T:NT], in0=psb[7][:, :].rearrange("p (j c) -> p j c", c=32)[:, :, 0:NS], scalar=0.5,
                in1=xT[:, :, T:NT], op0=ALU.mult, op1=ALU.add), [R_ps[7]] + R_x, R_x)

        load(0)
        load(1)
        up(0)
        for g in range(NG):
            if g + 2 < NG:
                load(g + 2)
            if g + 1 < NG:
                up(g + 1)
            down(g)


    BARS = set()

    def barrier():
        lasts = set()
        seen = set()
        for o in reversed(P.ops):
            if o.kind != "c":
                lasts.add(o.idx)
            elif o.eng not in seen:
                seen.add(o.eng)
                if o.idx not in BARS:
                    lasts.add(o.idx)
        for en in Prog.ENGS:
            o = P.op(en, lambda e: None, [], [])
            BARS.add(o.idx)
            o.deps = set(d for d in lasts if not (en == "pe" and P.ops[d].eng == "pe" and P.ops[d].kind == "c"))

    rot = {"ps": 0}

    def nextps():
        i = rot["ps"] % 4
        rot["ps"] += 1
        return i

    def proj_block(wsrc, c0w, slot, consumer, srcT, R_src, ncol=256):
        wload(wA[slot][:, :, 0:ncol], R_wA[slot], wsrc[:, c0w:c0w + ncol].rearrange("(k p) n -> p k n", p=128))
        for oc in range(ncol // 128):
            for ci, (c0, c1) in enumerate(CG):
                pi = nextps()
                n = c1 - c0

                def mm(e, oc=oc, c0=c0, c1=c1, pi=pi, n=n):
                    ins = None
                    for k in range(DC):
                        ins = e.matmul(psb[pi][:, 0:n], lhsT=wA[slot][:, k, oc * 128:(oc + 1) * 128], rhs=srcT[:, k, c0:c1],
                                       start=(k == 0), stop=(k == DC - 1))
                    return ins
                P.op("pe", mm, [R_wA[slot]] + R_src, [R_ps[pi]])
                consumer(oc, ci, c0, c1, psb[pi][:, 0:n], R_ps[pi])

    def vec_fm(dst, src1d, R_dst):
        dma("sp", dst, src1d.rearrange("(k p) -> p k", p=128), [], [R_dst], allow_slow_non_contiguous=True)

    def rg_layer():
        norm(1)
        barrier()
        off[0] = OV + 2 * 8192
        hgT = sb("hgT", [128, DC, NT], BF16)
        xwp = sb("xwp", [128, 2, 3 + T], F32)
        xws = sb("xws", [128, 2, 16], F32)
        xc = sb("xc", [128, 2, 1040], F32)
        xcb = sb("xcb", [128, 2, 1040], BF16)
        r2 = [sb("r2_%d" % i, [128, 1040], F32) for i in range(2)]
        a2 = [sb("a2_%d" % i, [128, 1040], F32) for i in range(2)]
        ib2 = [sb("ib2_%d" % i, [128, 1040], F32) for i in range(2)]
        h_ = sb("h_", [128, 1040], F32)
        wab = [sb("wab%d" % i, [128, 2, 256], BF16) for i in range(1)] * 2
        wib = [sb("wib%d" % i, [128, 2, 256], BF16) for i in range(1)] * 2
        cw = sb("cw", [128, DC, 4], F32)
        cb = sb("cb", [128, DC], F32)
        ba = sb("ba", [128, DC], F32)
        bi = sb("bi", [128, DC], F32)
        nsp = sb("nsp", [128, DC], F32)
        hin = sb("hin", [128, DC], F32)
        hl = sb("hl", [128, DC], F32)
        cvT = sb("cvT", [128, DC, 48], F32)
        hsT = sb("hsT", [128, DC, 16], F32)
        phT = sb("phT", [128, DC, 1], F32)
        shT = sb("shT", [128, DC, 16], F32)
        pcT = sb("pcT", [128, DC, 3], F32)
        scT = sb("scT", [128, DC, 16], F32)
        R = {n: Res(n) for n in "hgT xwp xws xc xcb r0 r1 a0 a1 ib0 ib1 h wab0 wib0 sm hin hl cvT hsT outs".split()}
        R["wab1"] = R["wab0"]
        R["wib1"] = R["wib0"]
        for j in range(4):
            dma("sp", cw[:, :, j], W["rg_conv_w"][0][j].rearrange("(k p) -> p k", p=128), [], [R["sm"]], allow_slow_non_contiguous=True)
        vec_fm(cb[:], W["rg_conv_b"][0], R["sm"])
        vec_fm(ba[:], W["rg_b_a"][0], R["sm"])
        vec_fm(bi[:], W["rg_b_i"][0], R["sm"])
        vec_fm(nsp[:], W["rg_lambda"][0], R["sm"])

        P.op("act", lambda e: e.activation(out=nsp[:], in_=nsp[:], func=AF.Exp, scale=-1.0), [R["sm"]], [R["sm"]])
        P.op("act", lambda e: e.activation(out=nsp[:], in_=nsp[:], func=AF.Ln, bias=1.0), [R["sm"]], [R["sm"]])
        P.op("dve", lambda e: e.tensor_scalar(out=nsp[:], in0=nsp[:], scalar1=-8.0, scalar2=None, op0=ALU.mult), [R["sm"]], [R["sm"]])
        P.op("pool", lambda e: e.memset(hgT[:, :, T + NSMP:NT], 0.0), [], [R["hgT"]])
        tm_to_fm(st_conv, 48, lambda k: cvT[:, k, :], stage[0], R_stage[0], 0, lambda q: [R["cvT"]])
        tm_to_fm(st_h, NSMP, lambda k: hsT[:, k, :], stage[1], R_stage[1], 1, lambda q: [R["hsT"]])
        barrier()
        wxs, wys = W["rg_w_x"][0], W["rg_w_y"][0]
        def stage_A(n):
            def cons_x(oc, ci, c0, c1, pap, Rp):
                if ci < 2:
                    P.op("act", lambda e: e.copy(out=xwp[:, oc, 3 + c0:3 + c1], in_=pap), [Rp], [R["xwp"]])
                else:
                    def f(e):
                        e.copy(out=xws[:, oc, :], in_=pap[:, 0:NSMP])
                        return e.copy(out=xwp[:, oc, 0:3], in_=pap[:, NSMP:NS])
                    P.op("act", f, [Rp], [R["xwp"], R["xws"]])
            proj_block(wxs, n * 256, 0, cons_x, uT, R_u)
            for oc in range(2):
                ch = 2 * n + oc

                def outs_x(e, ch=ch, oc=oc):
                    e.copy(out=pcT[:, ch, :], in_=xwp[:, oc, T:T + 3])
                    return e.copy(out=scT[:, ch, :], in_=xws[:, oc, :])
                P.op("act", outs_x, [R["xwp"], R["xws"]], [R["outs"]])

        stage_A(0)
        for n in range(8):
            wb = n % 2
            Rwa, Rwi = R["wab%d" % wb], R["wib%d" % wb]
            wload(wab[wb][:], Rwa, W["rg_w_a"][0][n].rearrange("(k p) m -> p k m", p=128))
            wload(wib[wb][:], Rwi, W["rg_w_i"][0][n].rearrange("(k p) m -> p k m", p=128))

            wload(wA[1][:], R_wA[1], wys[:, n * 256:(n + 1) * 256].rearrange("(k p) n -> p k n", p=128))
            for oc in range(2):
                ch = 2 * n + oc
                cv = cvT[:, ch, :].rearrange("p (s j) -> p s j", j=3)
                crd = [R["xwp"], R["xws"], R["sm"], R["cvT"]]

                def conv0(e, oc=oc, ch=ch):
                    e.tensor_scalar(out=xc[:, oc, 0:T], in0=xwp[:, oc, 3:3 + T], scalar1=cw[:, ch, 3:4], scalar2=cb[:, ch:ch + 1],
                                    op0=ALU.mult, op1=ALU.add)
                    return e.tensor_scalar(out=xc[:, oc, T:T + NSMP], in0=xws[:, oc, :], scalar1=cw[:, ch, 3:4], scalar2=cb[:, ch:ch + 1],
                                           op0=ALU.mult, op1=ALU.add)
                P.op("dve", conv0, crd, [R["xc"]])
                for j in range(3):
                    def convj(e, oc=oc, ch=ch, j=j, cv=cv):
                        e.scalar_tensor_tensor(out=xc[:, oc, 0:T], in0=xwp[:, oc, j:j + T], scalar=cw[:, ch, j:j + 1],
                                               in1=xc[:, oc, 0:T], op0=ALU.mult, op1=ALU.add)
                        return e.scalar_tensor_tensor(out=xc[:, oc, T:T + NSMP], in0=cv[:, :, j], scalar=cw[:, ch, j:j + 1],
                                                      in1=xc[:, oc, T:T + NSMP], op0=ALU.mult, op1=ALU.add)
                    P.op("dve", convj, crd + [R["xc"]], [R["xc"]])
                P.op("act", lambda e, oc=oc: e.copy(out=xcb[:, oc, :], in_=xc[:, oc, :]), [R["xc"]], [R["xcb"]])
            if n + 1 < 8:
                stage_A(n + 1)
            for oc in range(2):
                ch = 2 * n + oc
                a_, ib_, Ra, Rib = a2[oc], ib2[oc], R["a%d" % oc], R["ib%d" % oc]
                r_, Rr = r2[oc], R["r%d" % oc]
                for (gw, gb, gout, Rg, Rw) in ((wab[wb], ba, r_, Rr, Rwa), (wib[wb], bi, ib_, Rib, Rwi)):
                    for (c0, c1) in ((0, 512), (512, 1024), (T, T + NSMP)):
                        pi = nextps()
                        n_ = c1 - c0

                        def mm(e, gw=gw, oc=oc, c0=c0, c1=c1, pi=pi, n_=n_):
                            e.matmul(psb[pi][:, 0:n_], lhsT=gw[:, 0, oc * 128:(oc + 1) * 128], rhs=xcb[:, 0, c0:c1], start=True, stop=False)
                            return e.matmul(psb[pi][:, 0:n_], lhsT=gw[:, 1, oc * 128:(oc + 1) * 128], rhs=xcb[:, 1, c0:c1], start=False, stop=True)
                        P.op("pe", mm, [Rw, R["xcb"]], [R_ps[pi]])
                        P.op("act", lambda e, gout=gout, gb=gb, ch=ch, c0=c0, c1=c1, pi=pi, n_=n_: e.activation(
                            out=gout[:, c0:c1], in_=psb[pi][:, 0:n_], func=AF.Sigmoid, bias=gb[:, ch:ch + 1]), [R_ps[pi], R["sm"]], [Rg])
                P.op("dve", lambda e, ch=ch, r_=r_: e.tensor_scalar(out=r_[:], in0=r_[:], scalar1=nsp[:, ch:ch + 1], scalar2=None, op0=ALU.mult),
                     [Rr, R["sm"]], [Rr])
                P.op("act", lambda e, a_=a_, r_=r_: e.activation(out=a_[:], in_=r_[:], func=AF.Exp), [Rr], [Ra])
                P.op("act", lambda e, r_=r_: e.activation(out=r_[:], in_=r_[:], func=AF.Exp, scale=2.0), [Rr], [Rr])
                P.op("dve", lambda e, r_=r_: e.tensor_scalar(out=r_[:], in0=r_[:], scalar1=-1.0, scalar2=1.0, op0=ALU.mult, op1=ALU.add),
                     [Rr], [Rr])
                P.op("act", lambda e, r_=r_: e.activation(out=r_[:], in_=r_[:], func=AF.Sqrt), [Rr], [Rr])
                P.op("dve", lambda e, oc=oc, ib_=ib_: e.tensor_tensor(out=ib_[:], in0=ib_[:], in1=xc[:, oc, :], op=ALU.mult),
                     [Rib, R["xc"]], [Rib])
                P.op("dve", lambda e, ib_=ib_, r_=r_: e.tensor_tensor(out=ib_[:], in0=ib_[:], in1=r_[:], op=ALU.mult), [Rib, Rr], [Rib])
                P.op("dve", lambda e, a_=a_, ib_=ib_: e.tensor_tensor_scan(out=h_[:, 0:T], data0=a_[:, 0:T], data1=ib_[:, 0:T], initial=0.0,
                                                                         op0=ALU.mult, op1=ALU.add), [Ra, Rib], [R["h"]])
                P.op("act", lambda e, ch=ch: e.copy(out=hl[:, ch:ch + 1], in_=h_[:, T - 1:T]), [R["h"]], [R["hl"]])
            Rcc = Res("cch%d" % n)
            dma("sp", cc_h_in[n].ap(), hl[:, 2 * n:2 * n + 2], [R["hl"]], [Rcc])
            P.op("pool", lambda e, n=n: e.collective_compute("AllGather", ALU.bypass, replica_groups=PAIRS,
                                                              ins=[cc_h_in[n].ap().opt()], outs=[cc_h_out[n].ap().opt()]),
                 [Rcc], [Rcc], kind="cc")
            dma("sp", hin[:, 2 * n:2 * n + 2], cc_h_out[n].ap()[0:128, :], [Rcc], [R["hin"]])
            P.op("dve", lambda e, n=n: e.tensor_scalar(out=hin[:, 2 * n:2 * n + 2], in0=hin[:, 2 * n:2 * n + 2], scalar1=flag[:, 0:1],
                                                      scalar2=None, op0=ALU.mult), [R["hin"], R_g], [R["hin"]])
            for oc in range(2):
                ch = 2 * n + oc
                a_, ib_, Ra, Rib = a2[oc], ib2[oc], R["a%d" % oc], R["ib%d" % oc]
                r_, Rr = r2[oc], R["r%d" % oc]
                for ci, (c0, c1) in enumerate(CG):
                    pi = nextps()
                    n_ = c1 - c0

                    def mm(e, oc=oc, c0=c0, c1=c1, pi=pi, n_=n_):
                        ins = None
                        for k in range(DC):
                            ins = e.matmul(psb[pi][:, 0:n_], lhsT=wA[1][:, k, oc * 128:(oc + 1) * 128], rhs=uT[:, k, c0:c1],
                                           start=(k == 0), stop=(k == DC - 1))
                        return ins
                    P.op("pe", mm, [R_wA[1]] + R_u, [R_ps[pi]])
                    c1e = min(c1, T + NSMP)
                    P.op("act", lambda e, c0=c0, c1e=c1e, pi=pi, r_=r_: e.activation(out=r_[:, c0:c1e], in_=psb[pi][:, 0:c1e - c0],
                                                                                   func=AF.Gelu_apprx_tanh), [R_ps[pi]], [Rr])
                P.op("dve", lambda e, ch=ch, a_=a_, ib_=ib_: e.tensor_tensor_scan(out=h_[:, 0:T], data0=a_[:, 0:T], data1=ib_[:, 0:T],
                                                                                initial=hin[:, ch:ch + 1], op0=ALU.mult, op1=ALU.add),
                     [Ra, Rib, R["hin"]], [R["h"]])
                P.op("dve", lambda e, ch=ch, a_=a_: e.tensor_tensor(out=h_[:, T:T + NSMP], in0=a_[:, T:T + NSMP], in1=hsT[:, ch, :], op=ALU.mult),
                     [Ra, R["hsT"], R["h"]], [R["h"]])
                P.op("dve", lambda e, ib_=ib_: e.tensor_tensor(out=h_[:, T:T + NSMP], in0=h_[:, T:T + NSMP], in1=ib_[:, T:T + NSMP], op=ALU.add),
                     [R["h"], Rib], [R["h"]])

                def outs(e, ch=ch, oc=oc):
                    e.copy(out=phT[:, ch, :], in_=h_[:, T - 1:T])
                    return e.copy(out=shT[:, ch, :], in_=h_[:, T:T + NSMP])
                P.op("act", outs, [R["h"]], [R["outs"]])
                P.op("dve", lambda e, ch=ch, r_=r_: e.tensor_tensor(out=hgT[:, ch, 0:T + NSMP], in0=h_[:], in1=r_[:], op=ALU.mult),
                     [R["h"], Rr], [R["hgT"]])
        for jb in range(8):
            def cons_o(oc, ci, c0, c1, pap, Rp, jb=jb):
                j = 2 * jb + oc
                P.op("dve", lambda e: e.tensor_tensor(out=xT[:, j, c0:c1], in0=pap, in1=xT[:, j, c0:c1], op=ALU.add),
                     [Rp, R_x[j]], [R_x[j]])
            proj_block(W["rg_w_o"][0], jb * 256, jb % 2, cons_o, hgT, [R["hgT"]])
        barrier()
        fm_to_tm(lambda k: phT[:, k, :], 1, p_h, stage[0], R_stage[0], 0, lambda q: [R["outs"]])
        fm_to_tm(lambda k: shT[:, k, :], NSMP, s_h, stage[1], R_stage[1], 1, lambda q: [R["outs"]])
        fm_to_tm(lambda k: pcT[:, k, :], 3, p_conv, stage[0], R_stage[0], 0, lambda q: [R["outs"]])
        fm_to_tm(lambda k: scT[:, k, :], NSMP, s_conv[:, 2, :], stage[1], R_stage[1], 1, lambda q: [R["outs"]])
        dma("sp", s_conv[:, 0:2, :], st_conv.rearrange("(s j) d -> s j d", j=3)[:, 1:3, :], [], [R_out])
        barrier()

    def gla_layer():
        norm(4)
        barrier()
        off[0] = OV + 2 * 8192
        wO = [nc.alloc_sbuf_tensor_at("wO%d" % i, [128, 4, 1024], BF16, offset=OV + i * 8192) for i in range(2)]
        og = sb("og", [128, 4, NT], BF16)
        qf_off = off[0]
        qf = sb("qf", [128, NT], F32)
        Lc = sb("Lc", [128, NT], F32, at=qf_off)
        ke = sb("ke", [128, T], F32, at=qf_off)
        kf = sb("kf", [128, NT], F32)
        dd = sb("dd", [128, 2, T], F32)
        qi = sb("qi", [128, 2, T], BF16)
        ki = sb("ki", [128, 2, T], BF16)
        keT = sb("keT", [128, 8, 256], BF16)
        vt = sb("vt", [128, 8, 512], BF16)
        gr = sb("gr", [128, 4, NT], BF16)
        Sm = sb("Sm", [128, 2, 512], F32)
        Sb = sb("Sb", [128, 2, 512], BF16)
        scm = sb("scm", [128, 128], BF16)
        on = sb("on", [128, 512], F32)
        onb = sb("onb", [128, 512], F32)
        g1b = sb("g1b", [16, NT], BF16)
        wg1 = sb("wg1", [128, DC, 16], BF16)
        wg2 = sb("wg2", [16, 1024], BF16)
        bgn = sb("bgn", [128, 8], F32)
        sst = sb("sst", [128, 8], F32)
        als = sb("als", [128, 8, NSMP], F32)
        qm = sb("qm", [128, NSMP, 2, NSMP], BF16)
        kst = sb("kst", [NSMP, 256], BF16)
        vst = sb("vst", [NSMP, 512], F32)
        vm = [sb("vm%d" % i, [NSMP, 512], BF16) for i in range(2)]
        NS0 = 5
        S0 = [sb("S0_%d" % i, [128, 512], F32) for i in range(NS0)]
        S0b = [sb("S0b_%d" % i, [128, 512], BF16) for i in range(3)]
        scm2 = sb("scm2", [128, 128], BF16)
        osa = sb("osa", [NSMP, 512], F32)
        print("GLA sbuf free", SB0 + 212800 - off[0])
        on2 = sb("on2", [128, 512], F32)
        R = {n: Res(n) for n in ("og qf kf dd qi ki keT vt gr Sm0 Sm1 Sb scm0 scm1 on0 on1 sm g1b sst0 sst1 sst2 als qm kst vst vm0 vm1 "
                                 "S0b0 S0b1 S0b2 Sin sc0 sc1 osa").split()}
        R["Lc"] = R["qf"]
        R["ke"] = R["qf"]
        R_S0 = [Res("S0_%d" % i) for i in range(NS0)]
        R_Sm = [R["Sm0"], R["Sm1"]]
        wload(wg1[:], R["sm"], W["gla_w_g1"][0].rearrange("(k p) r -> p k r", p=128))
        wload(wg2[:], R["sm"], W["gla_w_g2"][0])
        vec_fm(bgn[:], W["gla_b_g"][0], R["sm"])
        dma("sp", onb[:], W["gla_onorm"].partition_broadcast(128), [], [R["sm"]])
        P.op("dve", lambda e: e.tensor_scalar(out=bgn[:], in0=bgn[:], scalar1=-1.0, scalar2=None, op0=ALU.mult), [R["sm"]], [R["sm"]])

        def consts(e):
            e.memset(og[:, :, T + NSMP:NT], 0.0)
            return e.memset(qm[:], 0.0)
        P.op("pool", consts, [], [R["og"], R["qm"]])
        for ci, (c0, c1) in enumerate(CG):
            pi = nextps()
            n_ = c1 - c0

            def mm(e, c0=c0, c1=c1, pi=pi, n_=n_):
                ins = None
                for k in range(DC):
                    ins = e.matmul(psb[pi][0:16, 0:n_], lhsT=wg1[:, k, :], rhs=uT[:, k, c0:c1], start=(k == 0), stop=(k == DC - 1))
                return ins
            P.op("pe", mm, [R["sm"]] + R_u, [R_ps[pi]])
            P.op("act", lambda e, c0=c0, c1=c1, pi=pi, n_=n_: e.copy(out=g1b[:, c0:c1], in_=psb[pi][0:16, 0:n_]), [R_ps[pi]], [R["g1b"]])
        wq, wk, wv, wr, wo = W["gla_w_q"][0], W["gla_w_k"][0], W["gla_w_v"][0], W["gla_w_r"][0], W["gla_w_o"][0]
        ws = {"i": 0}

        def slot():
            ws["i"] += 1
            return ws["i"] % 2

        def proj2(wsrc, c0w, sl, consumer, after):
            wload(wA[sl][:], R_wA[sl], wsrc[:, c0w:c0w + 256].rearrange("(k p) n -> p k n", p=128))
            for oc in range(2):
                for ci, (c0, c1) in enumerate(CG):
                    pi = nextps()
                    n = c1 - c0

                    def mm(e, oc=oc, c0=c0, c1=c1, pi=pi, n=n):
                        ins = None
                        for k in range(DC):
                            ins = e.matmul(psb[pi][:, 0:n], lhsT=wA[sl][:, k, oc * 128:(oc + 1) * 128], rhs=uT[:, k, c0:c1],
                                           start=(k == 0), stop=(k == DC - 1))
                        return ins
                    P.op("pe", mm, [R_wA[sl]] + R_u, [R_ps[pi]])
                    consumer(oc, ci, c0, c1, psb[pi][:, 0:n], R_ps[pi])
                after(oc)

        for hh in range(4):
            for oc in range(2):
                c8 = hh * 2 + oc
                for ci, (c0, c1) in enumerate(CG):
                    pi = nextps()
                    n_ = c1 - c0
                    P.op("pe", lambda e, c8=c8, c0=c0, c1=c1, pi=pi, n_=n_: e.matmul(
                        psb[pi][:, 0:n_], lhsT=wg2[:, c8 * 128:(c8 + 1) * 128], rhs=g1b[:, c0:c1], start=True, stop=True),
                        [R["sm"], R["g1b"]], [R_ps[pi]])
                    P.op("act", lambda e, c8=c8, c0=c0, c1=c1, pi=pi, n_=n_: e.activation(
                        out=Lc[:, c0:c1], in_=psb[pi][:, 0:n_], func=AF.Exp, scale=-1.0, bias=bgn[:, c8:c8 + 1]), [R_ps[pi], R["sm"]], [R["Lc"]])
                P.op("act", lambda e: e.activation(out=Lc[:, :], in_=Lc[:, :], func=AF.Ln, bias=1.0), [R["Lc"]], [R["Lc"]])
                P.op("act", lambda e, c8=c8: e.activation(out=als[:, c8, :], in_=Lc[:, T:T + NSMP], func=AF.Exp, scale=-1.0 / 16.0),
                     [R["Lc"]], [R["als"]])

                def cum(e):
                    ins = None
                    for c in range(8):
                        ins = e.tensor_tensor_scan(out=Lc[:, c * 128:(c + 1) * 128], data0=ones[:, :], data1=Lc[:, c * 128:(c + 1) * 128],
                                                   initial=0.0, op0=ALU.mult, op1=ALU.add)
                    return ins
                P.op("dve", cum, [R["Lc"], R_c], [R["Lc"]])
                P.op("act", lambda e, oc=oc: e.activation(out=dd[:, oc, :], in_=Lc[:, 0:T], func=AF.Exp, scale=-1.0 / 16.0),
                     [R["Lc"]], [R["dd"]])

            def cons_k(oc, ci, c0, c1, pap, Rp):
                P.op("act", lambda e: e.copy(out=kf[:, c0:c1], in_=pap), [Rp], [R["kf"]])

            def after_k(oc):
                P.op("dve", lambda e: e.reciprocal(out=ke[:], in_=dd[:, oc, :]), [R["dd"]], [R["ke"]])
                P.op("dve", lambda e: e.tensor_tensor(out=ke[:], in0=ke[:], in1=kf[:, 0:T], op=ALU.mult), [R["ke"], R["kf"]], [R["ke"]])
                P.op("dve", lambda e: e.tensor_copy(out=ki[:, oc, :], in_=ke[:]), [R["ke"]], [R["ki"]])

                def mk(e):
                    ins = None
                    for c in range(8):
                        ins = e.tensor_scalar(out=ke[:, c * 128:(c + 1) * 128], in0=ke[:, c * 128:(c + 1) * 128],
                                              scalar1=dd[:, oc, c * 128 + 127:c * 128 + 128], scalar2=None, op0=ALU.mult)
                    return ins
                P.op("dve", mk, [R["ke"], R["dd"]], [R["ke"]])
                for c in range(8):
                    pi = nextps()
                    P.op("pe", lambda e, c=c, pi=pi: e.transpose(psb[pi][:, 0:128], ke[:, c * 128:(c + 1) * 128], ident[:, :]),
                         [R["ke"], R_c], [R_ps[pi]])
                    P.op("act", lambda e, c=c, pi=pi: e.copy(out=keT[:, c, oc * 128:(oc + 1) * 128], in_=psb[pi][:, 0:128]),
                         [R_ps[pi]], [R["keT"]])
            sl = slot()
            proj2(wk, hh * 256, sl, cons_k, after_k)
            pi = nextps()

            def mmk(e, sl=sl, pi=pi):
                ins = None
                for k in range(DC):
                    ins = e.matmul(psb[pi][0:NS, 0:256], lhsT=uT[:, k, T:NT], rhs=wA[sl][:, k, :], start=(k == 0), stop=(k == DC - 1))
                return ins
            P.op("pe", mmk, [R_wA[sl]] + R_u, [R_ps[pi]])
            P.op("act", lambda e, pi=pi: e.copy(out=kst[:, :], in_=psb[pi][0:NSMP, 0:256]), [R_ps[pi]], [R["kst"]])
            for vb in range(2):
                sl = slot()
                wload(wA[sl][:], R_wA[sl], wv[:, hh * 512 + vb * 256: hh * 512 + (vb + 1) * 256].rearrange("(k p) n -> p k n", p=128))
                for tt in range(8):
                    pi = nextps()

                    def mmv(e, sl=sl, tt=tt, pi=pi):
                        ins = None
                        for k in range(DC):
                            ins = e.matmul(psb[pi][:, 0:256], lhsT=uT[:, k, tt * 128:(tt + 1) * 128], rhs=wA[sl][:, k, :],
                                           start=(k == 0), stop=(k == DC - 1))
                        return ins
                    P.op("pe", mmv, [R_wA[sl]] + R_u, [R_ps[pi]])
                    P.op("act", lambda e, tt=tt, vb=vb, pi=pi: e.copy(out=vt[:, tt, vb * 256:(vb + 1) * 256], in_=psb[pi][:, 0:256]),
                         [R_ps[pi]], [R["vt"]])
                pi = nextps()

                def mmvs(e, sl=sl, pi=pi):
                    ins = None
                    for k in range(DC):
                        ins = e.matmul(psb[pi][0:NS, 0:256], lhsT=uT[:, k, T:NT], rhs=wA[sl][:, k, :], start=(k == 0), stop=(k == DC - 1))
                    return ins
                P.op("pe", mmvs, [R_wA[sl]] + R_u, [R_ps[pi]])
                P.op("act", lambda e, pi=pi, vb=vb: e.copy(out=vst[:, vb * 256:(vb + 1) * 256], in_=psb[pi][0:NSMP, 0:256]),
                     [R_ps[pi]], [R["vst"]])
            P.op("pool", lambda e: e.memset(Sm[:], 0.0), [], R_Sm)
            SB_ = (7, 3)
            for c in range(8):
                for oc in range(2):
                    P.op("pe", lambda e, c=c, oc=oc: e.matmul(psb[SB_[oc]][:, :], lhsT=keT[:, c, oc * 128:(oc + 1) * 128], rhs=vt[:, c, :], start=True, stop=True),
                         [R["keT"], R["vt"]], [R_ps[SB_[oc]]])
                    P.op("dve", lambda e, c=c, oc=oc: e.scalar_tensor_tensor(
                        out=Sm[:, oc, :], in0=Sm[:, oc, :], scalar=dd[:, oc, c * 128 + 127:c * 128 + 128], in1=psb[SB_[oc]][:, :],
                        op0=ALU.mult, op1=ALU.add), [R_ps[SB_[oc]], R_Sm[oc], R["dd"]], [R_Sm[oc]])
            Rcc = Res("ccS%d" % hh)
            dma("sp", cc_S_in[hh].ap(), Sm[:].rearrange("p a v -> p (a v)"), R_Sm, [Rcc])
            P.op("pool", lambda e, hh=hh: e.collective_compute("AllGather", ALU.bypass, replica_groups=PAIRS,
                                                                ins=[cc_S_in[hh].ap().opt()], outs=[cc_S_out[hh].ap().opt()]),
                 [Rcc], [Rcc], kind="cc")
            def cons_q(oc, ci, c0, c1, pap, Rp):
                P.op("act", lambda e: e.activation(out=qf[:, c0:c1], in_=pap, func=AF.Copy, scale=1.0 / 16.0), [Rp], [R["qf"]])

            def after_q(oc):
                def mq(e):
                    e.tensor_tensor(out=qi[:, oc, :], in0=qf[:, 0:T], in1=dd[:, oc, :], op=ALU.mult)
                    ins = None
                    for b in range(NSMP):
                        ins = e.tensor_copy(out=qm[:, b, oc, b:b + 1], in_=qf[:, T + b:T + b + 1])
                    return ins
                P.op("dve", mq, [R["qf"], R["dd"]], [R["qi"], R["qm"]])
            proj2(wq, hh * 256, slot(), cons_q, after_q)
            for vb in range(2):
                def cons_r(oc, ci, c0, c1, pap, Rp, vb=vb):
                    P.op("act", lambda e: e.activation(out=gr[:, vb * 2 + oc, c0:c1], in_=pap, func=AF.Silu), [Rp], [R["gr"]])
                proj2(wr, hh * 512 + vb * 256, slot(), cons_r, lambda oc: None)
            dma("sp", Sm[:].rearrange("p a v -> p (a v)"), cc_S_out[hh].ap()[0:128, :], [Rcc], R_Sm)
            P.op("dve", lambda e: e.tensor_scalar(out=Sm[:].rearrange("p a v -> p (a v)"), in0=Sm[:].rearrange("p a v -> p (a v)"),
                                                  scalar1=flag[:, 0:1], scalar2=None, op0=ALU.mult), R_Sm + [R_g], R_Sm)
            P.op("act", lambda e: e.copy(out=Sb[:], in_=Sm[:]), R_Sm, [R["Sb"]])

            PF = 3

            def load_tile(j, hh=hh):
                bj, ocj = j // 2, j % 2
                dma("sp", S0[j % NS0][:], st_S[bj, hh, ocj * 128:(ocj + 1) * 128, :], [], [R_S0[j % NS0]])

            def sample_tile(idx, hh=hh):
                b, oc = idx // 2, idx % 2
                si, sbi, vi = idx % NS0, idx % 3, b % 2
                pob = idx % 2
                Rsb, Rvm = R["S0b%d" % sbi], R["vm%d" % vi]
                if oc == 0:
                    P.op("dve", lambda e: e.tensor_scalar(out=vm[vi][:], in0=vst[:], scalar1=ident[0:NSMP, b:b + 1],
                                                          scalar2=None, op0=ALU.mult), [R["vst"], R_c], [Rvm])
                if idx == 0:
                    for j in range(PF):
                        load_tile(j)
                if idx + PF < 2 * NSMP:
                    load_tile(idx + PF)
                P.op("pe", lambda e: e.matmul(psb[pob][:, :], lhsT=kst[:, oc * 128:(oc + 1) * 128], rhs=vm[vi][:], start=True, stop=True),
                     [R["kst"], Rvm], [R_ps[pob]])
                P.op("dve", lambda e: e.scalar_tensor_tensor(
                    out=S0[si][:], in0=S0[si][:], scalar=als[:, hh * 2 + oc, b:b + 1], in1=psb[pob][:, :], op0=ALU.mult, op1=ALU.add),
                    [R_ps[pob], R_S0[si], R["als"]], [R_S0[si]])
                dma("sp", s_S[b, hh, oc * 128:(oc + 1) * 128, :], S0[si][:], [R_S0[si]], [R_out])
                P.op("act", lambda e: e.copy(out=S0b[sbi][:], in_=S0[si][:]), [R_S0[si]], [Rsb])
                P.op("pe", lambda e: e.matmul(psb[2][0:NSMP, :], lhsT=qm[:, b, oc, :], rhs=S0b[sbi][:],
                                              start=(oc == 0), stop=(oc == 1)), [R["qm"], Rsb], [R_ps[2]])
                if oc == 1:
                    if b == 0:
                        P.op("dve", lambda e: e.tensor_copy(out=osa[:], in_=psb[2][0:NSMP, :]), [R_ps[2]], [R["osa"]])
                    else:
                        P.op("dve", lambda e: e.tensor_tensor(out=osa[:], in0=psb[2][0:NSMP, :], in1=osa[:], op=ALU.add),
                             [R_ps[2], R["osa"]], [R["osa"]])

            tile_i = [0]

            def some_tiles(k):
                for _ in range(k):
                    if tile_i[0] < 2 * NSMP:
                        sample_tile(tile_i[0])
                        tile_i[0] += 1

            scmv, onv = (scm, scm2), (on, on2)
            for c in range(8):
                cs = slice(c * 128, (c + 1) * 128)
                par = c % 2
                Rsc, Rscm, Ron, Rsst = R["sc%d" % par], R["scm%d" % par], R["on%d" % par], R["sst%d" % par]
                scp = psb[6][:, 256 + par * 128: 256 + (par + 1) * 128] if False else None
                pso = psb[4 + par]

                def mms(e, cs=cs, par=par):
                    e.matmul(psb[3][:, 384:512] if par else psb[7][:, 384:512], lhsT=ki[:, 0, cs], rhs=qi[:, 0, cs], start=True, stop=False)
                    return e.matmul(psb[3][:, 384:512] if par else psb[7][:, 384:512], lhsT=ki[:, 1, cs], rhs=qi[:, 1, cs], start=False, stop=True)
                def mms6(e, cs=cs, par=par):
                    e.matmul(psb[6][:, par * 128:(par + 1) * 128], lhsT=ki[:, 0, cs], rhs=qi[:, 0, cs], start=True, stop=False)
                    return e.matmul(psb[6][:, par * 128:(par + 1) * 128], lhsT=ki[:, 1, cs], rhs=qi[:, 1, cs], start=False, stop=True)
                P.op("pe", mms6, [R["ki"], R["qi"]], [Rsc])
                P.op("dve", lambda e, par=par: e.tensor_tensor(out=scmv[par][:], in0=psb[6][:, par * 128:(par + 1) * 128], in1=maskT[:], op=ALU.mult),
                     [Rsc, R_c], [Rscm])
                some_tiles(2)

                def mmo(e, cs=cs, c=c, par=par, pso=pso):
                    e.matmul(pso[:, :], lhsT=qi[:, 0, cs], rhs=Sb[:, 0, :], start=True, stop=False)
                    e.matmul(pso[:, :], lhsT=qi[:, 1, cs], rhs=Sb[:, 1, :], start=False, stop=False)
                    return e.matmul(pso[:, :], lhsT=scmv[par][:], rhs=vt[:, c, :], start=False, stop=True)
                P.op("pe", mmo, [R["qi"], R["Sb"], Rscm, R["vt"]], [R_ps[4 + par]])
                for oc in range(2):
                    P.op("pe", lambda e, c=c, oc=oc: e.matmul(psb[SB_[oc]][:, :], lhsT=keT[:, c, oc * 128:(oc + 1) * 128], rhs=vt[:, c, :], start=True, stop=True),
                         [R["keT"], R["vt"]], [R_ps[SB_[oc]]])
                    P.op("dve", lambda e, c=c, oc=oc: e.scalar_tensor_tensor(
                        out=Sm[:, oc, :], in0=Sm[:, oc, :], scalar=dd[:, oc, c * 128 + 127:c * 128 + 128], in1=psb[SB_[oc]][:, :],
                        op0=ALU.mult, op1=ALU.add), [R_ps[SB_[oc]], R_Sm[oc], R["dd"]], [R_Sm[oc]])
                P.op("pool", lambda e, par=par: e.memset(sst[:, par:par + 1], 0.0), [], [Rsst])
                P.op("act", lambda e, par=par, pso=pso: e.activation(out=onv[par][:], in_=pso[:, :], func=AF.Square, accum_out=sst[:, par:par + 1]),
                     [R_ps[4 + par]], [Ron, Rsst])
                P.op("act", lambda e, par=par: e.activation(out=sst[:, par:par + 1], in_=sst[:, par:par + 1], func=AF.Sqrt, scale=1.0 / 512.0, bias=EPS),
                     [Rsst], [Rsst])
                P.op("act", lambda e: e.copy(out=Sb[:], in_=Sm[:]), R_Sm, [R["Sb"]])
                some_tiles(0)
                P.op("dve", lambda e, par=par: e.reciprocal(out=sst[:, par:par + 1], in_=sst[:, par:par + 1]), [Rsst], [Rsst])
                P.op("dve", lambda e, par=par, pso=pso: e.scalar_tensor_tensor(out=onv[par][:], in0=pso[:, :], scalar=sst[:, par:par + 1], in1=onb[:],
                                                                               op0=ALU.mult, op1=ALU.mult),
                     [R_ps[4 + par], Rsst, Ron, R["sm"]], [Ron])
                some_tiles(2)

                def tro(e, par=par):
                    ins = None
                    for j in range(2):
                        ins = e.transpose(psb[6][:, 256 + j * 128:256 + (j + 1) * 128], onv[par][:, j * 128:(j + 1) * 128], ident[:, :])
                    return ins
                for hv in range(2):
                    Rtr = R["sst2"]
                    P.op("pe", lambda e, par=par, hv=hv: (e.transpose(psb[6][:, 256:384], onv[par][:, (2 * hv) * 128:(2 * hv + 1) * 128], ident[:, :]),
                                                          e.transpose(psb[6][:, 384:512], onv[par][:, (2 * hv + 1) * 128:(2 * hv + 2) * 128], ident[:, :]))[1],
                         [Ron, R_c], [Rtr])
                    P.op("dve", lambda e, c=c, hv=hv: e.tensor_tensor(
                        out=og[:, 2 * hv:2 * hv + 2, c * 128:(c + 1) * 128], in0=psb[6][:, 256:512].rearrange("p (j t) -> p j t", t=128),
                        in1=gr[:, 2 * hv:2 * hv + 2, c * 128:(c + 1) * 128], op=ALU.mult), [Rtr, R["gr"]], [R["og"]])
                some_tiles(0)
            some_tiles(2 * NSMP)
            dma("sp", p_S[hh].rearrange("(a p) v -> p a v", p=128), Sm[:], R_Sm, [R_out])
            P.op("pool", lambda e: e.memset(sst[:, 2:3], 0.0), [], [R["sst0"]])
            P.op("act", lambda e: e.activation(out=on[0:NSMP, :], in_=osa[:], func=AF.Square, accum_out=sst[0:NSMP, 2:3]),
                 [R["osa"]], [R["on0"], R["sst0"]])
            P.op("act", lambda e: e.activation(out=sst[0:NSMP, 2:3], in_=sst[0:NSMP, 2:3], func=AF.Sqrt, scale=1.0 / 512.0, bias=EPS),
                 [R["sst0"]], [R["sst0"]])
            P.op("dve", lambda e: e.reciprocal(out=sst[0:NSMP, 2:3], in_=sst[0:NSMP, 2:3]), [R["sst0"]], [R["sst0"]])
            P.op("dve", lambda e: e.scalar_tensor_tensor(out=on[0:NSMP, :], in0=osa[:], scalar=sst[0:NSMP, 2:3],
                                                         in1=onb[0:NSMP, :], op0=ALU.mult, op1=ALU.mult),
                 [R["osa"], R["sst0"], R["on0"], R["sm"]], [R["on0"]])

            def tros(e):
                ins = None
                for j in range(4):
                    ins = e.transpose(psb[3][:, j * 128:j * 128 + NSMP], on[0:NSMP, j * 128:(j + 1) * 128], ident[0:NSMP, 0:NSMP])
                return ins
            P.op("pe", tros, [R["on0"], R_c], [R_ps[3]])
            P.op("dve", lambda e: e.tensor_tensor(
                out=og[:, :, T:T + NSMP], in0=psb[3][:, :].rearrange("p (j t) -> p j t", t=128)[:, :, 0:NSMP],
                in1=gr[:, :, T:T + NSMP], op=ALU.mult), [R_ps[3], R["gr"]], [R["og"]])
            for half in range(2):
                sl = slot()
                wload(wO[sl][:], R_wA[sl], wo[hh * 512:(hh + 1) * 512, half * 1024:(half + 1) * 1024].rearrange("(k p) n -> p k n", p=128))
                for jj in range(8):
                    j = half * 8 + jj
                    for ci, (c0, c1) in enumerate(CG):
                        pi = nextps()
                        n_ = c1 - c0

                        def mmw(e, sl=sl, jj=jj, c0=c0, c1=c1, pi=pi, n_=n_):
                            ins = None
                            for k in range(4):
                                ins = e.matmul(psb[pi][:, 0:n_], lhsT=wO[sl][:, k, jj * 128:(jj + 1) * 128], rhs=og[:, k, c0:c1],
                                               start=(k == 0), stop=(k == 3))
                            return ins
                        P.op("pe", mmw, [R_wA[sl], R["og"]], [R_ps[pi]])
                        P.op("dve", lambda e, j=j, c0=c0, c1=c1, pi=pi, n_=n_: e.tensor_tensor(
                            out=xT[:, j, c0:c1], in0=psb[pi][:, 0:n_], in1=xT[:, j, c0:c1], op=ALU.add), [R_ps[pi], R_x[j]], [R_x[j]])
        barrier()

    import os
    STG = os.environ.get("KSTAGE", "all")
    ffn(0, W["ffn1_w1"], W["ffn1_w3"], W["ffn1_w2"], 0)
    rg_layer()
    if STG == "all":
        ffn(0, W["ffn2_w1"], W["ffn2_w3"], W["ffn2_w2"], 2)
        ffn(1, W["ffn1_w1"], W["ffn1_w3"], W["ffn1_w2"], 3)
    if STG in ("all", "gla"):
        gla_layer()
    if STG == "all":
        ffn(1, W["ffn2_w1"], W["ffn2_w3"], W["ffn2_w2"], 5)
    barrier()
    def emit_out(ci):
        if ci < 2:
            for tt in range(4 * ci, 4 * ci + 4):
                st = tt % 2
                fm_to_tm(lambda k, tt=tt: xT[:, k, tt * 128:(tt + 1) * 128], 128, y_p[tt * 128:(tt + 1) * 128, :],
                         stage[st], R_stage[st], st, lambda q: [R_x[q * 4 + j] for j in range(4)])
        else:
            fm_to_tm(lambda k: xT[:, k, T:T + NSMP], NSMP, y_s, stage[0], R_stage[0], 0, lambda q: [R_x[q * 4 + j] for j in range(4)])

    norm(6, out_fp32=True, after_group=emit_out)
    P.op("sp", lambda e: None, R_outs, [])
    P.emit(nc, es)
    es.close()
    return nc, P


_CACHE = {}


def kernel(**inputs):
    if "nc" not in _CACHE:
        _CACHE["nc"] = build_nc()
    nc, _ = _CACHE["nc"]
    f32 = lambda a: np.ascontiguousarray(np.asarray(a, dtype=np.float32))
    x_prompt = f32(inputs["x_prompt"])
    x_sample = f32(inputs["x_sample"])
    sh = f32(inputs["state_rglru_h"])
    scv = f32(inputs["state_rglru_conv"])
    sS = f32(inputs["state_gla_S"])
    wts = {n: f32(inputs[n]) for n in WNAMES}
    in_maps = []
    for c in range(8):
        b, half = c // 2, c % 2
        m = dict(wts)
        m["xp"] = x_prompt[b, half * T:(half + 1) * T]
        xs = np.zeros((NS, D), np.float32)
        xs[:NSMP] = x_sample[c * NSMP:(c + 1) * NSMP, 0]
        if half == 1:
            xs[NSMP:] = x_prompt[b, T - 3:T]
        m["xsm"] = xs
        m["st_h"] = sh[0, c * NSMP:(c + 1) * NSMP]
        m["st_conv"] = scv[0, c * NSMP:(c + 1) * NSMP].reshape(NSMP * 3, D)
        m["st_S"] = sS[0, c * NSMP:(c + 1) * NSMP]
        m["flag"] = np.full((128, 1), float(half), np.float32)
        in_maps.append(m)
    res = run_bass_kernel_spmd(nc, in_maps, core_ids=list(range(8)))
    r = res.results
    y_prompt = np.stack([np.concatenate([r[2 * b]["y_p"], r[2 * b + 1]["y_p"]], 0) for b in range(4)], 0)
    y_sample = np.concatenate([r[c]["y_s"] for c in range(8)], 0)[:, None, :]
    p_h = np.stack([r[2 * b + 1]["p_h"][0] for b in range(4)], 0)[None]
    p_conv = np.stack([r[2 * b + 1]["p_conv"] for b in range(4)], 0)[None]
    p_S = np.stack([r[2 * b + 1]["p_S"] for b in range(4)], 0)[None]
    s_h = np.concatenate([r[c]["s_h"] for c in range(8)], 0)[None]
    s_conv = np.concatenate([r[c]["s_conv"] for c in range(8)], 0)[None]
    s_S = np.concatenate([r[c]["s_S"] for c in range(8)], 0)[None]
    return (y_prompt, y_sample, p_h, p_conv, p_S, s_h, s_conv, s_S)
```

```python
import numpy as np
from contextlib import ExitStack
import concourse.bass as bass
import concourse.mybir as mybir
from concourse.bass_utils import run_bass_kernel_spmd

F32 = mybir.dt.float32
BF16 = mybir.dt.bfloat16
ALU = mybir.AluOpType
AF = mybir.ActivationFunctionType

D = 2048
DC = 16
T = 1024
NSMP = 16
NHALO = 3
NS = NSMP + NHALO
NT = T + NS
FF = 5632
NG = FF // 256
CG = [(0, 512), (512, 1024), (1024, NT)]
EPS = 1e-6
PAIRS = [[0, 1], [2, 3], [4, 5], [6, 7]]


class Res:
    __slots__ = ("name", "w", "r")

    def __init__(self, name):
        self.name = name
        self.w = None
        self.r = []


class Op:
    __slots__ = ("eng", "fn", "deps", "idx", "signal", "kind", "sem", "semval", "prev")


class Prog:
    ENGS = ("pe", "act", "dve", "pool", "sp")
    NDMASEM = 12

    def __init__(self):
        self.ops = []

    def op(self, eng, fn, reads=(), writes=(), kind="c"):
        o = Op()
        o.eng = eng
        o.fn = fn
        o.kind = kind
        o.idx = len(self.ops)
        o.signal = False
        o.sem = None
        o.semval = 0
        o.prev = None
        deps = set()
        for r in reads:
            if r.w is not None:
                deps.add(r.w)
        for w in writes:
            if w.w is not None:
                deps.add(w.w)
            deps.update(w.r)
        for r in reads:
            r.r.append(o.idx)
        for w in writes:
            w.w = o.idx
            w.r = []
        deps.discard(o.idx)
        if eng == "pe" and kind == "c":
            deps = {d for d in deps if not (self.ops[d].eng == "pe" and self.ops[d].kind == "c")}
        o.deps = deps
        self.ops.append(o)
        return o

    def emit(self, nc, es):
        ops = self.ops
        for o in ops:
            for d in o.deps:
                ops[d].signal = True
        LIM = 1000
        esem = {e: [] for e in self.ENGS}
        dsem = {e: [es.enter_context(nc.semaphore("d_%s%d" % (e, i))) for i in range(self.NDMASEM)]
                for e in ("sp", "pool")}
        cnt = {e: 0 for e in self.ENGS}
        dcnt = {e: 0 for e in dsem}
        last_on_sem = {}
        ncc = 0
        for o in ops:
            if o.kind == "dma":
                k = dcnt[o.eng]
                dcnt[o.eng] += 1
                o.sem = dsem[o.eng][k % self.NDMASEM]
                o.semval = 16 * (k // self.NDMASEM + 1)
                o.prev = last_on_sem.get((o.eng, k % self.NDMASEM))
                last_on_sem[(o.eng, k % self.NDMASEM)] = o
            elif o.kind == "cc":
                o.sem = es.enter_context(nc.semaphore("cc%d" % ncc))
                ncc += 1
                o.semval = 1
            elif o.signal:
                cnt[o.eng] += 1
                ep = (cnt[o.eng] - 1) // LIM
                if ep >= len(esem[o.eng]):
                    esem[o.eng].append(es.enter_context(nc.semaphore("s_%s%d" % (o.eng, ep))))
                o.sem = esem[o.eng][ep]
                o.semval = (cnt[o.eng] - 1) % LIM + 1
        self.counts = dict(cnt)
        self.dcounts = dict(dcnt)
        per_eng = {e: [o for o in ops if o.eng == e] for e in self.ENGS}

        def run(eng_name, e):
            waited = {}
            for o in per_eng[eng_name]:
                need = {}
                xs = [ops[d] for d in o.deps]
                if o.kind == "dma" and o.prev is not None:
                    xs.append(o.prev)
                for x in xs:
                    key = id(x.sem)
                    if key not in need or need[key][1] < x.semval:
                        need[key] = (x.sem, x.semval)
                for key, (s, v) in need.items():
                    if waited.get(key, 0) >= v:
                        continue
                    e.wait_ge(s, v)
                    waited[key] = v
                ins = o.fn(e)
                if o.kind == "dma":
                    ins.then_inc(o.sem, 16)
                elif o.kind == "cc":
                    ins.then_inc(o.sem, 1)
                elif o.signal:
                    ins.then_inc(o.sem, 1)

        with nc.Block() as block:
            @block.tensor
            def _(e):
                run("pe", e)

            @block.scalar
            def _(e):
                run("act", e)

            @block.vector
            def _(e):
                run("dve", e)

            @block.gpsimd
            def _(e):
                run("pool", e)

            @block.sync
            def _(e):
                run("sp", e)


WNAMES = ["ln_ffn1", "ffn1_w1", "ffn1_w3", "ffn1_w2", "ln_mix", "ln_ffn2", "ffn2_w1", "ffn2_w3", "ffn2_w2",
          "rg_w_y", "rg_w_x", "rg_conv_w", "rg_conv_b", "rg_w_a", "rg_b_a", "rg_w_i", "rg_b_i", "rg_lambda", "rg_w_o",
          "gla_w_q", "gla_w_k", "gla_w_v", "gla_w_g1", "gla_w_g2", "gla_b_g", "gla_w_r", "gla_onorm", "gla_w_o",
          "ln_final"]
WSHAPES = {
    "ln_ffn1": [2, D], "ffn1_w1": [2, D, FF], "ffn1_w3": [2, D, FF], "ffn1_w2": [2, FF, D], "ln_mix": [2, D],
    "ln_ffn2": [2, D], "ffn2_w1": [2, D, FF], "ffn2_w3": [2, D, FF], "ffn2_w2": [2, FF, D],
    "rg_w_y": [1, D, D], "rg_w_x": [1, D, D], "rg_conv_w": [1, 4, D], "rg_conv_b": [1, D], "rg_w_a": [1, 8, 256, 256],
    "rg_b_a": [1, D], "rg_w_i": [1, 8, 256, 256], "rg_b_i": [1, D], "rg_lambda": [1, D], "rg_w_o": [1, D, D],
    "gla_w_q": [1, D, 1024], "gla_w_k": [1, D, 1024], "gla_w_v": [1, D, D], "gla_w_g1": [1, D, 16],
    "gla_w_g2": [1, 16, 1024], "gla_b_g": [1, 1024], "gla_w_r": [1, D, D], "gla_onorm": [1, 512], "gla_w_o": [1, D, D],
    "ln_final": [D],
}


def build_nc():
    nc = bass.Bass("TRN2", target_bir_lowering=False)
    din = lambda n, s: nc.dram_tensor(n, s, F32, kind="ExternalInput").ap()
    dout = lambda n, s: nc.dram_tensor(n, s, F32, kind="ExternalOutput").ap()
    xp = din("xp", [T, D])
    xsm = din("xsm", [NS, D])
    st_h = din("st_h", [NSMP, D])
    st_conv = din("st_conv", [NSMP * 3, D])
    st_S = din("st_S", [NSMP, 4, 256, 512])
    flag_d = din("flag", [128, 1])
    W = {n: din(n, WSHAPES[n]) for n in WNAMES}
    y_p = dout("y_p", [T, D])
    y_s = dout("y_s", [NSMP, D])
    p_h = dout("p_h", [1, D])
    p_conv = dout("p_conv", [3, D])
    p_S = dout("p_S", [4, 256, 512])
    s_h = dout("s_h", [NSMP, D])
    s_conv = dout("s_conv", [NSMP, 3, D])
    s_S = dout("s_S", [NSMP, 4, 256, 512])
    cc_h_in = [nc.dram_tensor("cc_h_in%d" % i, [128, 2], F32) for i in range(8)]
    cc_h_out = [nc.dram_tensor("cc_h_out%d" % i, [256, 2], F32) for i in range(8)]
    cc_S_in = [nc.dram_tensor("cc_S_in%d" % i, [128, 1024], F32) for i in range(4)]
    cc_S_out = [nc.dram_tensor("cc_S_out%d" % i, [256, 1024], F32) for i in range(4)]

    P = Prog()
    es = ExitStack()
    SB0 = 16512
    off = [SB0]

    def sb(name, shape, dt, at=None):
        nbytes = int(np.prod(shape[1:])) * (4 if dt == F32 else 2)
        nbytes = (nbytes + 31) // 32 * 32
        if at is None:
            o = off[0]
            off[0] += nbytes
        else:
            o = at
        assert o + nbytes <= SB0 + 212800, (name, o, nbytes)
        return nc.alloc_sbuf_tensor_at(name, list(shape), dt, offset=o)

    xT = sb("xT", [128, DC, NT], F32)
    uT = sb("uT", [128, DC, NT], BF16)
    ident = sb("ident", [128, 128], F32)
    ones = sb("ones", [128, 128], F32)
    maskT = sb("maskT", [128, 128], F32)
    gains = sb("gains", [128, 7, DC], F32)
    flag = sb("flag_s", [128, 1], F32)
    scr = sb("scr", [128, 8], F32)
    OV = off[0]
    psb = [nc.alloc_psum_tensor("ps%d" % i, [128, 512], F32) for i in range(8)]
    R_ps = [Res("ps%d" % i) for i in range(8)]
    R_x = [Res("xT%d" % k) for k in range(DC)]
    R_uc = [[Res("uT%d_%d" % (k, ci)) for k in range(DC)] for ci in range(3)]
    R_u = [r for l in R_uc for r in l]
    R_rstdc = [Res("rstd%d" % ci) for ci in range(3)]
    R_c = Res("consts")
    R_out = Res("outs")

    R_outs = []

    def dma(eng, out, in_, reads, writes, **kw):
        writes = list(writes)
        if R_out in writes:
            r = Res("out%d" % len(R_outs))
            R_outs.append(r)
            writes = [w for w in writes if w is not R_out] + [r]
        P.op(eng, lambda e: e.dma_start(out=out, in_=in_, **kw), reads, writes, kind="dma")

    P.op("pool", lambda e: e.memset(ident[:], 1.0), [], [R_c])
    P.op("pool", lambda e: e.affine_select(out=ident[:], in_=ident[:], pattern=[[-1, 128]], compare_op=ALU.is_equal,
                                           fill=0.0, base=0, channel_multiplier=1), [R_c], [R_c])
    P.op("pool", lambda e: e.memset(maskT[:], 1.0), [R_c], [R_c])
    P.op("pool", lambda e: e.affine_select(out=maskT[:], in_=maskT[:], pattern=[[1, 128]], compare_op=ALU.is_ge,
                                           fill=0.0, base=0, channel_multiplier=-1), [R_c], [R_c])
    P.op("pool", lambda e: e.memset(ones[:], 1.0), [R_c], [R_c])
    gsrc = [W["ln_ffn1"][0], W["ln_mix"][0], W["ln_ffn2"][0], W["ln_ffn1"][1], W["ln_mix"][1], W["ln_ffn2"][1],
            W["ln_final"]]
    R_g = Res("gains")
    for i, g in enumerate(gsrc):
        dma("sp", gains[:, i, :], g.rearrange("(k p) -> p k", p=128), [], [R_g], allow_slow_non_contiguous=True)
    dma("sp", flag[:], flag_d, [], [R_g])

    def tm_to_fm(src_dram, nrows, dst_fn, stage, R_stage, ps_i, writes):
        dma("sp", stage[0:nrows, :], src_dram, [], [R_stage])
        for q in range(4):
            def tr(e, q=q):
                ins = None
                for j in range(4):
                    k = q * 4 + j
                    ins = e.transpose(psb[ps_i][:, j * 128:j * 128 + nrows], stage[0:nrows, k * 128:(k + 1) * 128],
                                      ident[0:nrows, 0:nrows])
                return ins
            P.op("pe", tr, [R_stage, R_c], [R_ps[ps_i]])
            def ev(e, q=q):
                ins = None
                for j in range(4):
                    ins = e.copy(out=dst_fn(q * 4 + j), in_=psb[ps_i][:, j * 128:j * 128 + nrows])
                return ins
            P.op("act", ev, [R_ps[ps_i]], writes(q))

    def fm_to_tm(src_fn, nrows, dst_dram, stage, R_stage, ps_i, reads):
        for q in range(4):
            def tr(e, q=q):
                ins = None
                for j in range(4):
                    ins = e.transpose(psb[ps_i][0:nrows, j * 128:(j + 1) * 128], src_fn(q * 4 + j), ident[:, :])
                return ins
            P.op("pe", tr, list(reads(q)) + [R_c], [R_ps[ps_i]])
            P.op("act", lambda e, q=q: e.copy(out=stage[0:nrows, q * 512:(q + 1) * 512], in_=psb[ps_i][0:nrows, :]),
                 [R_ps[ps_i]], [R_stage])
        dma("sp", dst_dram, stage[0:nrows, :], [R_stage], [R_out])

    off[0] = OV
    wA = [sb("wA%d" % i, [128, DC, 256], BF16) for i in range(3)]
    wB = [sb("wB%d" % i, [128, DC, 256], BF16) for i in range(3)]
    wC = [sb("wC%d" % i, [128, 2, D], BF16) for i in range(3)]
    hT = [sb("hT%d" % i, [128, 2, NT], BF16) for i in range(2)]
    sil = [sb("sil%d" % i, [128, 512], F32) for i in range(2)]
    sq = [sb("sq%d" % i, [128, 512], F32) for i in range(4)]
    rstd = sb("rstd", [128, NT], F32)
    OV2 = off[0]
    stage = [sb("stage%d" % i, [128, D], F32, at=OV + i * 8192) for i in range(2)]
    R_wA = [Res("wA%d" % i) for i in range(3)]
    R_wB = [Res("wB%d" % i) for i in range(3)]
    R_wC = [Res("wC%d" % i) for i in range(3)]
    R_hT = [Res("hT%d" % i) for i in range(2)]
    R_sil = [Res("sil%d" % i) for i in range(2)]
    R_sq = [Res("sq%d" % i) for i in range(4)]
    R_stage = R_wA[:2]
    cnt = {"w": 0, "sq": 0, "pab": 0, "dn": 0}

    for tt in range(T // 128):
        st = tt % 2
        tm_to_fm(xp[tt * 128:(tt + 1) * 128, :], 128, lambda k, tt=tt: xT[:, k, tt * 128:(tt + 1) * 128],
                 stage[st], R_stage[st], st, lambda q: [R_x[q * 4 + j] for j in range(4)])
    tm_to_fm(xsm, NS, lambda k: xT[:, k, T:NT], stage[0], R_stage[0], 0, lambda q: [R_x[q * 4 + j] for j in range(4)])

    def norm(gi, out_fp32=None, after_group=None):
        for ci, (c0, c1) in enumerate(CG):
            n = c1 - c0
            pi = ci % 2
            for k in range(DC):
                s = cnt["sq"] % 4
                cnt["sq"] += 1
                P.op("act", lambda e, k=k, s=s, c0=c0, c1=c1, n=n: e.activation(out=sq[s][:, 0:n], in_=xT[:, k, c0:c1], func=AF.Square),
                     [R_x[k]], [R_sq[s]])
                P.op("pe", lambda e, k=k, s=s, n=n, pi=pi: e.matmul(psb[pi][:, 0:n], lhsT=ones[:, :], rhs=sq[s][:, 0:n],
                                                                    start=(k == 0), stop=(k == DC - 1)),
                     [R_sq[s], R_c], [R_ps[pi]])
            P.op("act", lambda e, c0=c0, c1=c1, n=n, pi=pi: e.activation(out=rstd[:, c0:c1], in_=psb[pi][:, 0:n], func=AF.Sqrt,
                                                                         scale=1.0 / D, bias=EPS),
                 [R_ps[pi]], [R_rstdc[ci]])
            P.op("dve", lambda e, c0=c0, c1=c1: e.reciprocal(out=rstd[:, c0:c1], in_=rstd[:, c0:c1]), [R_rstdc[ci]], [R_rstdc[ci]])
            for k in range(DC):
                if out_fp32 is None:
                    P.op("dve", lambda e, k=k, c0=c0, c1=c1: e.scalar_tensor_tensor(
                        out=uT[:, k, c0:c1], in0=xT[:, k, c0:c1], scalar=gains[:, gi, k:k + 1], in1=rstd[:, c0:c1],
                        op0=ALU.mult, op1=ALU.mult), [R_x[k], R_rstdc[ci], R_g], [R_uc[ci][k]])
                else:
                    P.op("dve", lambda e, k=k, c0=c0, c1=c1: e.scalar_tensor_tensor(
                        out=xT[:, k, c0:c1], in0=xT[:, k, c0:c1], scalar=gains[:, gi, k:k + 1], in1=rstd[:, c0:c1],
                        op0=ALU.mult, op1=ALU.mult), [R_x[k], R_rstdc[ci], R_g], [R_x[k]])
            if after_group is not None:
                after_group(ci)

    def wload(dst, R_dst, src_ap):
        dma("pool", dst, src_ap, [], [R_dst])

    def ffn(l, w1, w3, w2, gi):
        norm(gi)
        w1l, w3l, w2l = w1[l], w3[l], w2[l]

        def load(g):
            s = g % 3
            wload(wA[s][:], R_wA[s], w1l[:, g * 256:(g + 1) * 256].rearrange("(k p) n -> p k n", p=128))
            wload(wB[s][:], R_wB[s], w3l[:, g * 256:(g + 1) * 256].rearrange("(k p) n -> p k n", p=128))
            wload(wC[s][:], R_wC[s], w2l[g * 256:(g + 1) * 256, :].rearrange("(c p) n -> p c n", p=128))

        def up(g):
            s = g % 3
            hb = g % 2
            for fl in range(2):
                for ci, (c0, c1) in enumerate(CG):
                    n = c1 - c0
                    if ci < 2:
                        i = cnt["pab"] % 2
                        cnt["pab"] += 1
                        pa, pb, Ra, Rb = psb[i][:, 0:n], psb[2 + i][:, 0:n], R_ps[i], R_ps[2 + i]
                        so, Rs = sil[i][:, 0:n], R_sil[i]
                    else:
                        pa, pb, Ra, Rb = psb[6][:, 0:n], psb[6][:, 32:32 + n], R_ps[6], R_ps[6]
                        so, Rs = sil[0][:, 0:n], R_sil[0]

                    def mm(e, s=s, fl=fl, c0=c0, c1=c1, pa=pa, pb=pb):
                        ins = None
                        for k in range(DC):
                            e.matmul(pa, lhsT=wA[s][:, k, fl * 128:(fl + 1) * 128], rhs=uT[:, k, c0:c1],
                                     start=(k == 0), stop=(k == DC - 1))
                        for k in range(DC):
                            ins = e.matmul(pb, lhsT=wB[s][:, k, fl * 128:(fl + 1) * 128], rhs=uT[:, k, c0:c1],
                                           start=(k == 0), stop=(k == DC - 1))
                        return ins
                    P.op("pe", mm, [R_wA[s], R_wB[s]] + R_uc[ci], list({Ra, Rb}))
                    P.op("act", lambda e, so=so, pa=pa: e.activation(out=so, in_=pa, func=AF.Silu), [Ra], [Rs])
                    P.op("dve", lambda e, hb=hb, fl=fl, c0=c0, c1=c1, so=so, pb=pb: e.tensor_tensor(
                        out=hT[hb][:, fl, c0:c1], in0=so, in1=pb, op=ALU.mult), [Rs, Rb], [R_hT[hb]])

        def down(g):
            s = g % 3
            hb = g % 2
            for j in range(DC):
                for ci, (c0, c1) in enumerate(CG[:2]):
                    di = cnt["dn"] % 6
                    cnt["dn"] += 1
                    bi_ = (4, 5, 0, 1, 2, 3)[di]
                    i = di % 2
                    py = psb[bi_]

                    def mm(e, j=j, c0=c0, c1=c1, py=py):
                        e.matmul(py[:, :], lhsT=wC[s][:, 0, j * 128:(j + 1) * 128], rhs=hT[hb][:, 0, c0:c1], start=True, stop=False)
                        return e.matmul(py[:, :], lhsT=wC[s][:, 1, j * 128:(j + 1) * 128], rhs=hT[hb][:, 1, c0:c1], start=False, stop=True)
                    P.op("pe", mm, [R_wC[s], R_hT[hb]], [R_ps[bi_]])
                    if i == 0:
                        P.op("dve", lambda e, j=j, c0=c0, c1=c1, py=py: e.scalar_tensor_tensor(
                            out=xT[:, j, c0:c1], in0=py[:, :], scalar=0.5, in1=xT[:, j, c0:c1], op0=ALU.mult, op1=ALU.add),
                            [R_ps[bi_], R_x[j]], [R_x[j]])
                    else:
                        q = cnt["sq"] % 4
                        cnt["sq"] += 1
                        P.op("act", lambda e, py=py, q=q: e.activation(out=sq[q][:, :], in_=py[:, :], func=AF.Copy, scale=0.5),
                             [R_ps[bi_]], [R_sq[q]])
                        P.op("pool", lambda e, j=j, c0=c0, c1=c1, q=q: e.tensor_tensor(
                            out=xT[:, j, c0:c1], in0=sq[q][:, :], in1=xT[:, j, c0:c1], op=ALU.add), [R_sq[q], R_x[j]], [R_x[j]])

            def mms(e):
                ins = None
                for j in range(DC):
                    e.matmul(psb[7][:, j * 32:j * 32 + NS], lhsT=wC[s][:, 0, j * 128:(j + 1) * 128], rhs=hT[hb][:, 0, T:NT], start=True, stop=False)
                    ins = e.matmul(psb[7][:, j * 32:j * 32 + NS], lhsT=wC[s][:, 1, j * 128:(j + 1) * 128], rhs=hT[hb][:, 1, T:NT], start=False, stop=True)
                return ins
            P.op("pe", mms, [R_wC[s], R_hT[hb]], [R_ps[7]])
            P.op("dve", lambda e: e.scalar_tensor_tensor(
                out=xT[:, :, T:NT], in0=psb[7][:, :].rearrange("p (j c) -> p j c", c=32)[:, :, 0:NS], scalar=0.5,
                in1=xT[:, :, T:NT], op0=ALU.mult, op1=ALU.add), [R_ps[7]] + R_x, R_x)

        load(0)
        load(1)
        up(0)
        for g in range(NG):
            if g + 2 < NG:
                load(g + 2)
            if g + 1 < NG:
                up(g + 1)
            down(g)


    BARS = set()

    def barrier():
        lasts = set()
        seen = set()
        for o in reversed(P.ops):
            if o.kind != "c":
                lasts.add(o.idx)
            elif o.eng not in seen:
                seen.add(o.eng)
                if o.idx not in BARS:
                    lasts.add(o.idx)
        for en in Prog.ENGS:
            o = P.op(en, lambda e: None, [], [])
            BARS.add(o.idx)
            o.deps = set(d for d in lasts if not (en == "pe" and P.ops[d].eng == "pe" and P.ops[d].kind == "c"))

    rot = {"ps": 0}

    def nextps():
        i = rot["ps"] % 4
        rot["ps"] += 1
        return i

    def proj_block(wsrc, c0w, slot, consumer, srcT, R_src, ncol=256):
        wload(wA[slot][:, :, 0:ncol], R_wA[slot], wsrc[:, c0w:c0w + ncol].rearrange("(k p) n -> p k n", p=128))
        for oc in range(ncol // 128):
            for ci, (c0, c1) in enumerate(CG):
                pi = nextps()
                n = c1 - c0

                def mm(e, oc=oc, c0=c0, c1=c1, pi=pi, n=n):
                    ins = None
                    for k in range(DC):
                        ins = e.matmul(psb[pi][:, 0:n], lhsT=wA[slot][:, k, oc * 128:(oc + 1) * 128], rhs=srcT[:, k, c0:c1],
                                       start=(k == 0), stop=(k == DC - 1))
                    return ins
                P.op("pe", mm, [R_wA[slot]] + R_src, [R_ps[pi]])
                consumer(oc, ci, c0, c1, psb[pi][:, 0:n], R_ps[pi])

    def vec_fm(dst, src1d, R_dst):
        dma("sp", dst, src1d.rearrange("(k p) -> p k", p=128), [], [R_dst], allow_slow_non_contiguous=True)

    def rg_layer():
        norm(1)
        barrier()
        off[0] = OV + 2 * 8192
        hgT = sb("hgT", [128, DC, NT], BF16)
        xwp = sb("xwp", [128, 2, 3 + T], F32)
        xws = sb("xws", [128, 2, 16], F32)
        xc = sb("xc", [128, 2, 1040], F32)
        xcb = sb("xcb", [128, 2, 1040], BF16)
        r2 = [sb("r2_%d" % i, [128, 1040], F32) for i in range(2)]
        a2 = [sb("a2_%d" % i, [128, 1040], F32) for i in range(2)]
        ib2 = [sb("ib2_%d" % i, [128, 1040], F32) for i in range(2)]
        h_ = sb("h_", [128, 1040], F32)
        wab = [sb("wab%d" % i, [128, 2, 256], BF16) for i in range(1)] * 2
        wib = [sb("wib%d" % i, [128, 2, 256], BF16) for i in range(1)] * 2
        cw = sb("cw", [128, DC, 4], F32)
        cb = sb("cb", [128, DC], F32)
        ba = sb("ba", [128, DC], F32)
        bi = sb("bi", [128, DC], F32)
        nsp = sb("nsp", [128, DC], F32)
        hin = sb("hin", [128, DC], F32)
        hl = sb("hl", [128, DC], F32)
        cvT = sb("cvT", [128, DC, 48], F32)
        hsT = sb("hsT", [128, DC, 16], F32)
        phT = sb("phT", [128, DC, 1], F32)
        shT = sb("shT", [128, DC, 16], F32)
        pcT = sb("pcT", [128, DC, 3], F32)
        scT = sb("scT", [128, DC, 16], F32)
        R = {n: Res(n) for n in "hgT xwp xws xc xcb r0 r1 a0 a1 ib0 ib1 h wab0 wib0 sm hin hl cvT hsT outs".split()}
        R["wab1"] = R["wab0"]
        R["wib1"] = R["wib0"]
        for j in range(4):
            dma("sp", cw[:, :, j], W["rg_conv_w"][0][j].rearrange("(k p) -> p k", p=128), [], [R["sm"]], allow_slow_non_contiguous=True)
        vec_fm(cb[:], W["rg_conv_b"][0], R["sm"])
        vec_fm(ba[:], W["rg_b_a"][0], R["sm"])
        vec_fm(bi[:], W["rg_b_i"][0], R["sm"])
        vec_fm(nsp[:], W["rg_lambda"][0], R["sm"])

        P.op("act", lambda e: e.activation(out=nsp[:], in_=nsp[:], func=AF.Exp, scale=-1.0), [R["sm"]], [R["sm"]])
        P.op("act", lambda e: e.activation(out=nsp[:], in_=nsp[:], func=AF.Ln, bias=1.0), [R["sm"]], [R["sm"]])
        P.op("dve", lambda e: e.tensor_scalar(out=nsp[:], in0=nsp[:], scalar1=-8.0, scalar2=None, op0=ALU.mult), [R["sm"]], [R["sm"]])
        P.op("pool", lambda e: e.memset(hgT[:, :, T + NSMP:NT], 0.0), [], [R["hgT"]])
        tm_to_fm(st_conv, 48, lambda k: cvT[:, k, :], stage[0], R_stage[0], 0, lambda q: [R["cvT"]])
        tm_to_fm(st_h, NSMP, lambda k: hsT[:, k, :], stage[1], R_stage[1], 1, lambda q: [R["hsT"]])
        barrier()
        wxs, wys = W["rg_w_x"][0], W["rg_w_y"][0]
        def stage_A(n):
            def cons_x(oc, ci, c0, c1, pap, Rp):
                if ci < 2:
                    P.op("act", lambda e: e.copy(out=xwp[:, oc, 3 + c0:3 + c1], in_=pap), [Rp], [R["xwp"]])
                else:
                    def f(e):
                        e.copy(out=xws[:, oc, :], in_=pap[:, 0:NSMP])
                        return e.copy(out=xwp[:, oc, 0:3], in_=pap[:, NSMP:NS])
                    P.op("act", f, [Rp], [R["xwp"], R["xws"]])
            proj_block(wxs, n * 256, 0, cons_x, uT, R_u)
            for oc in range(2):
                ch = 2 * n + oc

                def outs_x(e, ch=ch, oc=oc):
                    e.copy(out=pcT[:, ch, :], in_=xwp[:, oc, T:T + 3])
                    return e.copy(out=scT[:, ch, :], in_=xws[:, oc, :])
                P.op("act", outs_x, [R["xwp"], R["xws"]], [R["outs"]])

        stage_A(0)
        for n in range(8):
            wb = n % 2
            Rwa, Rwi = R["wab%d" % wb], R["wib%d" % wb]
            wload(wab[wb][:], Rwa, W["rg_w_a"][0][n].rearrange("(k p) m -> p k m", p=128))
            wload(wib[wb][:], Rwi, W["rg_w_i"][0][n].rearrange("(k p) m -> p k m", p=128))

            wload(wA[1][:], R_wA[1], wys[:, n * 256:(n + 1) * 256].rearrange("(k p) n -> p k n", p=128))
            for oc in range(2):
                ch = 2 * n + oc
                cv = cvT[:, ch, :].rearrange("p (s j) -> p s j", j=3)
                crd = [R["xwp"], R["xws"], R["sm"], R["cvT"]]

                def conv0(e, oc=oc, ch=ch):
                    e.tensor_scalar(out=xc[:, oc, 0:T], in0=xwp[:, oc, 3:3 + T], scalar1=cw[:, ch, 3:4], scalar2=cb[:, ch:ch + 1],
                                    op0=ALU.mult, op1=ALU.add)
                    return e.tensor_scalar(out=xc[:, oc, T:T + NSMP], in0=xws[:, oc, :], scalar1=cw[:, ch, 3:4], scalar2=cb[:, ch:ch + 1],
                                           op0=ALU.mult, op1=ALU.add)
                P.op("dve", conv0, crd, [R["xc"]])
                for j in range(3):
                    def convj(e, oc=oc, ch=ch, j=j, cv=cv):
                        e.scalar_tensor_tensor(out=xc[:, oc, 0:T], in0=xwp[:, oc, j:j + T], scalar=cw[:, ch, j:j + 1],
                                               in1=xc[:, oc, 0:T], op0=ALU.mult, op1=ALU.add)
                        return e.scalar_tensor_tensor(out=xc[:, oc, T:T + NSMP], in0=cv[:, :, j], scalar=cw[:, ch, j:j + 1],
                                                      in1=xc[:, oc, T:T + NSMP], op0=ALU.mult, op1=ALU.add)
                    P.op("dve", convj, crd + [R["xc"]], [R["xc"]])
                P.op("act", lambda e, oc=oc: e.copy(out=xcb[:, oc, :], in_=xc[:, oc, :]), [R["xc"]], [R["xcb"]])
            if n + 1 < 8:
                stage_A(n + 1)
            for oc in range(2):
                ch = 2 * n + oc
                a_, ib_, Ra, Rib = a2[oc], ib2[oc], R["a%d" % oc], R["ib%d" % oc]
                r_, Rr = r2[oc], R["r%d" % oc]
                for (gw, gb, gout, Rg, Rw) in ((wab[wb], ba, r_, Rr, Rwa), (wib[wb], bi, ib_, Rib, Rwi)):
                    for (c0, c1) in ((0, 512), (512, 1024), (T, T + NSMP)):
                        pi = nextps()
                        n_ = c1 - c0

                        def mm(e, gw=gw, oc=oc, c0=c0, c1=c1, pi=pi, n_=n_):
                            e.matmul(psb[pi][:, 0:n_], lhsT=gw[:, 0, oc * 128:(oc + 1) * 128], rhs=xcb[:, 0, c0:c1], start=True, stop=False)
                            return e.matmul(psb[pi][:, 0:n_], lhsT=gw[:, 1, oc * 128:(oc + 1) * 128], rhs=xcb[:, 1, c0:c1], start=False, stop=True)
                        P.op("pe", mm, [Rw, R["xcb"]], [R_ps[pi]])
                        P.op("act", lambda e, gout=gout, gb=gb, ch=ch, c0=c0, c1=c1, pi=pi, n_=n_: e.activation(
                            out=gout[:, c0:c1], in_=psb[pi][:, 0:n_], func=AF.Sigmoid, bias=gb[:, ch:ch + 1]), [R_ps[pi], R["sm"]], [Rg])
                P.op("dve", lambda e, ch=ch, r_=r_: e.tensor_scalar(out=r_[:], in0=r_[:], scalar1=nsp[:, ch:ch + 1], scalar2=None, op0=ALU.mult),
                     [Rr, R["sm"]], [Rr])
                P.op("act", lambda e, a_=a_, r_=r_: e.activation(out=a_[:], in_=r_[:], func=AF.Exp), [Rr], [Ra])
                P.op("act", lambda e, r_=r_: e.activation(out=r_[:], in_=r_[:], func=AF.Exp, scale=2.0), [Rr], [Rr])
                P.op("dve", lambda e, r_=r_: e.tensor_scalar(out=r_[:], in0=r_[:], scalar1=-1.0, scalar2=1.0, op0=ALU.mult, op1=ALU.add),
                     [Rr], [Rr])
                P.op("act", lambda e, r_=r_: e.activation(out=r_[:], in_=r_[:], func=AF.Sqrt), [Rr], [Rr])
                P.op("dve", lambda e, oc=oc, ib_=ib_: e.tensor_tensor(out=ib_[:], in0=ib_[:], in1=xc[:, oc, :], op=ALU.mult),
                     [Rib, R["xc"]], [Rib])
                P.op("dve", lambda e, ib_=ib_, r_=r_: e.tensor_tensor(out=ib_[:], in0=ib_[:], in1=r_[:], op=ALU.mult), [Rib, Rr], [Rib])
                P.op("dve", lambda e, a_=a_, ib_=ib_: e.tensor_tensor_scan(out=h_[:, 0:T], data0=a_[:, 0:T], data1=ib_[:, 0:T], initial=0.0,
                                                                         op0=ALU.mult, op1=ALU.add), [Ra, Rib], [R["h"]])
                P.op("act", lambda e, ch=ch: e.copy(out=hl[:, ch:ch + 1], in_=h_[:, T - 1:T]), [R["h"]], [R["hl"]])
            Rcc = Res("cch%d" % n)
            dma("sp", cc_h_in[n].ap(), hl[:, 2 * n:2 * n + 2], [R["hl"]], [Rcc])
            P.op("pool", lambda e, n=n: e.collective_compute("AllGather", ALU.bypass, replica_groups=PAIRS,
                                                              ins=[cc_h_in[n].ap().opt()], outs=[cc_h_out[n].ap().opt()]),
                 [Rcc], [Rcc], kind="cc")
            dma("sp", hin[:, 2 * n:2 * n + 2], cc_h_out[n].ap()[0:128, :], [Rcc], [R["hin"]])
            P.op("dve", lambda e, n=n: e.tensor_scalar(out=hin[:, 2 * n:2 * n + 2], in0=hin[:, 2 * n:2 * n + 2], scalar1=flag[:, 0:1],
                                                      scalar2=None, op0=ALU.mult), [R["hin"], R_g], [R["hin"]])
            for oc in range(2):
                ch = 2 * n + oc
                a_, ib_, Ra, Rib = a2[oc], ib2[oc], R["a%d" % oc], R["ib%d" % oc]
                r_, Rr = r2[oc], R["r%d" % oc]
                for ci, (c0, c1) in enumerate(CG):
                    pi = nextps()
                    n_ = c1 - c0

                    def mm(e, oc=oc, c0=c0, c1=c1, pi=pi, n_=n_):
                        ins = None
                        for k in range(DC):
                            ins = e.matmul(psb[pi][:, 0:n_], lhsT=wA[1][:, k, oc * 128:(oc + 1) * 128], rhs=uT[:, k, c0:c1],
                                           start=(k == 0), stop=(k == DC - 1))
                        return ins
                    P.op("pe", mm, [R_wA[1]] + R_u, [R_ps[pi]])
                    c1e = min(c1, T + NSMP)
                    P.op("act", lambda e, c0=c0, c1e=c1e, pi=pi, r_=r_: e.activation(out=r_[:, c0:c1e], in_=psb[pi][:, 0:c1e - c0],
                                                                                   func=AF.Gelu_apprx_tanh), [R_ps[pi]], [Rr])
                P.op("dve", lambda e, ch=ch, a_=a_, ib_=ib_: e.tensor_tensor_scan(out=h_[:, 0:T], data0=a_[:, 0:T], data1=ib_[:, 0:T],
                                                                                initial=hin[:, ch:ch + 1], op0=ALU.mult, op1=ALU.add),
                     [Ra, Rib, R["hin"]], [R["h"]])
                P.op("dve", lambda e, ch=ch, a_=a_: e.tensor_tensor(out=h_[:, T:T + NSMP], in0=a_[:, T:T + NSMP], in1=hsT[:, ch, :], op=ALU.mult),
                     [Ra, R["hsT"], R["h"]], [R["h"]])
                P.op("dve", lambda e, ib_=ib_: e.tensor_tensor(out=h_[:, T:T + NSMP], in0=h_[:, T:T + NSMP], in1=ib_[:, T:T + NSMP], op=ALU.add),
                     [R["h"], Rib], [R["h"]])

                def outs(e, ch=ch, oc=oc):
                    e.copy(out=phT[:, ch, :], in_=h_[:, T - 1:T])
                    return e.copy(out=shT[:, ch, :], in_=h_[:, T:T + NSMP])
                P.op("act", outs, [R["h"]], [R["outs"]])
                P.op("dve", lambda e, ch=ch, r_=r_: e.tensor_tensor(out=hgT[:, ch, 0:T + NSMP], in0=h_[:], in1=r_[:], op=ALU.mult),
                     [R["h"], Rr], [R["hgT"]])
        for jb in range(8):
            def cons_o(oc, ci, c0, c1, pap, Rp, jb=jb):
                j = 2 * jb + oc
                P.op("dve", lambda e: e.tensor_tensor(out=xT[:, j, c0:c1], in0=pap, in1=xT[:, j, c0:c1], op=ALU.add),
                     [Rp, R_x[j]], [R_x[j]])
            proj_block(W["rg_w_o"][0], jb * 256, jb % 2, cons_o, hgT, [R["hgT"]])
        barrier()
        fm_to_tm(lambda k: phT[:, k, :], 1, p_h, stage[0], R_stage[0], 0, lambda q: [R["outs"]])
        fm_to_tm(lambda k: shT[:, k, :], NSMP, s_h, stage[1], R_stage[1], 1, lambda q: [R["outs"]])
        fm_to_tm(lambda k: pcT[:, k, :], 3, p_conv, stage[0], R_stage[0], 0, lambda q: [R["outs"]])
        fm_to_tm(lambda k: scT[:, k, :], NSMP, s_conv[:, 2, :], stage[1], R_stage[1], 1, lambda q: [R["outs"]])
        dma("sp", s_conv[:, 0:2, :], st_conv.rearrange("(s j) d -> s j d", j=3)[:, 1:3, :], [], [R_out])
        barrier()

    def gla_layer():
        norm(4)
        barrier()
        off[0] = OV + 2 * 8192
        wO = [nc.alloc_sbuf_tensor_at("wO%d" % i, [128, 4, 1024], BF16, offset=OV + i * 8192) for i in range(2)]
        og = sb("og", [128, 4, NT], BF16)
        qf_off = off[0]
        qf = sb("qf", [128, NT], F32)
        Lc = sb("Lc", [128, NT], F32, at=qf_off)
        ke = sb("ke", [128, T], F32, at=qf_off)
        kf = sb("kf", [128, NT], F32)
        dd = sb("dd", [128, 2, T], F32)
        qi = sb("qi", [128, 2, T], BF16)
        ki = sb("ki", [128, 2, T], BF16)
        keT = sb("keT", [128, 8, 256], BF16)
        vt = sb("vt", [128, 8, 512], BF16)
        gr = sb("gr", [128, 4, NT], BF16)
        Sm = sb("Sm", [128, 2, 512], F32)
        Sb = sb("Sb", [128, 2, 512], BF16)
        scm = sb("scm", [128, 128], BF16)
        on = sb("on", [128, 512], F32)
        onb = sb("onb", [128, 512], F32)
        g1b = sb("g1b", [16, NT], BF16)
        wg1 = sb("wg1", [128, DC, 16], BF16)
        wg2 = sb("wg2", [16, 1024], BF16)
        bgn = sb("bgn", [128, 8], F32)
        sst = sb("sst", [128, 8], F32)
        als = sb("als", [128, 8, NSMP], F32)
        qm = sb("qm", [128, NSMP, 2, NSMP], BF16)
        kst = sb("kst", [NSMP, 256], BF16)
        vst = sb("vst", [NSMP, 512], F32)
        vm = [sb("vm%d" % i, [NSMP, 512], BF16) for i in range(2)]
        NS0 = 5
        S0 = [sb("S0_%d" % i, [128, 512], F32) for i in range(NS0)]
        S0b = [sb("S0b_%d" % i, [128, 512], BF16) for i in range(3)]
        scm2 = sb("scm2", [128, 128], BF16)
        osa = sb("osa", [NSMP, 512], F32)
        print("GLA sbuf free", SB0 + 212800 - off[0])
        on2 = sb("on2", [128, 512], F32)
        R = {n: Res(n) for n in ("og qf kf dd qi ki keT vt gr Sm0 Sm1 Sb scm0 scm1 on0 on1 sm g1b sst0 sst1 sst2 als qm kst vst vm0 vm1 "
                                 "S0b0 S0b1 S0b2 Sin sc0 sc1 osa").split()}
        R["Lc"] = R["qf"]
        R["ke"] = R["qf"]
        R_S0 = [Res("S0_%d" % i) for i in range(NS0)]
        R_Sm = [R["Sm0"], R["Sm1"]]
        wload(wg1[:], R["sm"], W["gla_w_g1"][0].rearrange("(k p) r -> p k r", p=128))
        wload(wg2[:], R["sm"], W["gla_w_g2"][0])
        vec_fm(bgn[:], W["gla_b_g"][0], R["sm"])
        dma("sp", onb[:], W["gla_onorm"].partition_broadcast(128), [], [R["sm"]])
        P.op("dve", lambda e: e.tensor_scalar(out=bgn[:], in0=bgn[:], scalar1=-1.0, scalar2=None, op0=ALU.mult), [R["sm"]], [R["sm"]])

        def consts(e):
            e.memset(og[:, :, T + NSMP:NT], 0.0)
            return e.memset(qm[:], 0.0)
        P.op("pool", consts, [], [R["og"], R["qm"]])
        for ci, (c0, c1) in enumerate(CG):
            pi = nextps()
            n_ = c1 - c0

            def mm(e, c0=c0, c1=c1, pi=pi, n_=n_):
                ins = None
                for k in range(DC):
                    ins = e.matmul(psb[pi][0:16, 0:n_], lhsT=wg1[:, k, :], rhs=uT[:, k, c0:c1], start=(k == 0), stop=(k == DC - 1))
                return ins
            P.op("pe", mm, [R["sm"]] + R_u, [R_ps[pi]])
            P.op("act", lambda e, c0=c0, c1=c1, pi=pi, n_=n_: e.copy(out=g1b[:, c0:c1], in_=psb[pi][0:16, 0:n_]), [R_ps[pi]], [R["g1b"]])
        wq, wk, wv, wr, wo = W["gla_w_q"][0], W["gla_w_k"][0], W["gla_w_v"][0], W["gla_w_r"][0], W["gla_w_o"][0]
        ws = {"i": 0}

        def slot():
            ws["i"] += 1
            return ws["i"] % 2

        def proj2(wsrc, c0w, sl, consumer, after):
            wload(wA[sl][:], R_wA[sl], wsrc[:, c0w:c0w + 256].rearrange("(k p) n -> p k n", p=128))
            for oc in range(2):
                for ci, (c0, c1) in enumerate(CG):
                    pi = nextps()
                    n = c1 - c0

                    def mm(e, oc=oc, c0=c0, c1=c1, pi=pi, n=n):
                        ins = None
                        for k in range(DC):
                            ins = e.matmul(psb[pi][:, 0:n], lhsT=wA[sl][:, k, oc * 128:(oc + 1) * 128], rhs=uT[:, k, c0:c1],
                                           start=(k == 0), stop=(k == DC - 1))
                        return ins
                    P.op("pe", mm, [R_wA[sl]] + R_u, [R_ps[pi]])
                    consumer(oc, ci, c0, c1, psb[pi][:, 0:n], R_ps[pi])
                after(oc)

        for hh in range(4):
            for oc in range(2):
                c8 = hh * 2 + oc
                for ci, (c0, c1) in enumerate(CG):
                    pi = nextps()
                    n_ = c1 - c0
                    P.op("pe", lambda e, c8=c8, c0=c0, c1=c1, pi=pi, n_=n_: e.matmul(
                        psb[pi][:, 0:n_], lhsT=wg2[:, c8 * 128:(c8 + 1) * 128], rhs=g1b[:, c0:c1], start=True, stop=True),
                        [R["sm"], R["g1b"]], [R_ps[pi]])
                    P.op("act", lambda e, c8=c8, c0=c0, c1=c1, pi=pi, n_=n_: e.activation(
                        out=Lc[:, c0:c1], in_=psb[pi][:, 0:n_], func=AF.Exp, scale=-1.0, bias=bgn[:, c8:c8 + 1]), [R_ps[pi], R["sm"]], [R["Lc"]])
                P.op("act", lambda e: e.activation(out=Lc[:, :], in_=Lc[:, :], func=AF.Ln, bias=1.0), [R["Lc"]], [R["Lc"]])
                P.op("act", lambda e, c8=c8: e.activation(out=als[:, c8, :], in_=Lc[:, T:T + NSMP], func=AF.Exp, scale=-1.0 / 16.0),
                     [R["Lc"]], [R["als"]])

                def cum(e):
                    ins = None
                    for c in range(8):
                        ins = e.tensor_tensor_scan(out=Lc[:, c * 128:(c + 1) * 128], data0=ones[:, :], data1=Lc[:, c * 128:(c + 1) * 128],
                                                   initial=0.0, op0=ALU.mult, op1=ALU.add)
                    return ins
                P.op("dve", cum, [R["Lc"], R_c], [R["Lc"]])
                P.op("act", lambda e, oc=oc: e.activation(out=dd[:, oc, :], in_=Lc[:, 0:T], func=AF.Exp, scale=-1.0 / 16.0),
                     [R["Lc"]], [R["dd"]])

            def cons_k(oc, ci, c0, c1, pap, Rp):
                P.op("act", lambda e: e.copy(out=kf[:, c0:c1], in_=pap), [Rp], [R["kf"]])

            def after_k(oc):
                P.op("dve", lambda e: e.reciprocal(out=ke[:], in_=dd[:, oc, :]), [R["dd"]], [R["ke"]])
                P.op("dve", lambda e: e.tensor_tensor(out=ke[:], in0=ke[:], in1=kf[:, 0:T], op=ALU.mult), [R["ke"], R["kf"]], [R["ke"]])
                P.op("dve", lambda e: e.tensor_copy(out=ki[:, oc, :], in_=ke[:]), [R["ke"]], [R["ki"]])

                def mk(e):
                    ins = None
                    for c in range(8):
                        ins = e.tensor_scalar(out=ke[:, c * 128:(c + 1) * 128], in0=ke[:, c * 128:(c + 1) * 128],
                                              scalar1=dd[:, oc, c * 128 + 127:c * 128 + 128], scalar2=None, op0=ALU.mult)
                    return ins
                P.op("dve", mk, [R["ke"], R["dd"]], [R["ke"]])
                for c in range(8):
                    pi = nextps()
                    P.op("pe", lambda e, c=c, pi=pi: e.transpose(psb[pi][:, 0:128], ke[:, c * 128:(c + 1) * 128], ident[:, :]),
                         [R["ke"], R_c], [R_ps[pi]])
                    P.op("act", lambda e, c=c, pi=pi: e.copy(out=keT[:, c, oc * 128:(oc + 1) * 128], in_=psb[pi][:, 0:128]),
                         [R_ps[pi]], [R["keT"]])
            sl = slot()
            proj2(wk, hh * 256, sl, cons_k, after_k)
            pi = nextps()

            def mmk(e, sl=sl, pi=pi):
                ins = None
                for k in range(DC):
                    ins = e.matmul(psb[pi][0:NS, 0:256], lhsT=uT[:, k, T:NT], rhs=wA[sl][:, k, :], start=(k == 0), stop=(k == DC - 1))
                return ins
            P.op("pe", mmk, [R_wA[sl]] + R_u, [R_ps[pi]])
            P.op("act", lambda e, pi=pi: e.copy(out=kst[:, :], in_=psb[pi][0:NSMP, 0:256]), [R_ps[pi]], [R["kst"]])
            for vb in range(2):
                sl = slot()
                wload(wA[sl][:], R_wA[sl], wv[:, hh * 512 + vb * 256: hh * 512 + (vb + 1) * 256].rearrange("(k p) n -> p k n", p=128))
                for tt in range(8):
                    pi = nextps()

                    def mmv(e, sl=sl, tt=tt, pi=pi):
                        ins = None
                        for k in range(DC):
                            ins = e.matmul(psb[pi][:, 0:256], lhsT=uT[:, k, tt * 128:(tt + 1) * 128], rhs=wA[sl][:, k, :],
                                           start=(k == 0), stop=(k == DC - 1))
                        return ins
                    P.op("pe", mmv, [R_wA[sl]] + R_u, [R_ps[pi]])
                    P.op("act", lambda e, tt=tt, vb=vb, pi=pi: e.copy(out=vt[:, tt, vb * 256:(vb + 1) * 256], in_=psb[pi][:, 0:256]),
                         [R_ps[pi]], [R["vt"]])
                pi = nextps()

                def mmvs(e, sl=sl, pi=pi):
                    ins = None
                    for k in range(DC):
                        ins = e.matmul(psb[pi][0:NS, 0:256], lhsT=uT[:, k, T:NT], rhs=wA[sl][:, k, :], start=(k == 0), stop=(k == DC - 1))
                    return ins
                P.op("pe", mmvs, [R_wA[sl]] + R_u, [R_ps[pi]])
                P.op("act", lambda e, pi=pi, vb=vb: e.copy(out=vst[:, vb * 256:(vb + 1) * 256], in_=psb[pi][0:NSMP, 0:256]),
                     [R_ps[pi]], [R["vst"]])
            P.op("pool", lambda e: e.memset(Sm[:], 0.0), [], R_Sm)
            SB_ = (7, 3)
            for c in range(8):
                for oc in range(2):
                    P.op("pe", lambda e, c=c, oc=oc: e.matmul(psb[SB_[oc]][:, :], lhsT=keT[:, c, oc * 128:(oc + 1) * 128], rhs=vt[:, c, :], start=True, stop=True),
                         [R["keT"], R["vt"]], [R_ps[SB_[oc]]])
                    P.op("dve", lambda e, c=c, oc=oc: e.scalar_tensor_tensor(
                        out=Sm[:, oc, :], in0=Sm[:, oc, :], scalar=dd[:, oc, c * 128 + 127:c * 128 + 128], in1=psb[SB_[oc]][:, :],
                        op0=ALU.mult, op1=ALU.add), [R_ps[SB_[oc]], R_Sm[oc], R["dd"]], [R_Sm[oc]])
            Rcc = Res("ccS%d" % hh)
            dma("sp", cc_S_in[hh].ap(), Sm[:].rearrange("p a v -> p (a v)"), R_Sm, [Rcc])
            P.op("pool", lambda e, hh=hh: e.collective_compute("AllGather", ALU.bypass, replica_groups=PAIRS,
                                                                ins=[cc_S_in[hh].ap().opt()], outs=[cc_S_out[hh].ap().opt()]),
                 [Rcc], [Rcc], kind="cc")
            def cons_q(oc, ci, c0, c1, pap, Rp):
                P.op("act", lambda e: e.activation(out=qf[:, c0:c1], in_=pap, func=AF.Copy, scale=1.0 / 16.0), [Rp], [R["qf"]])

            def after_q(oc):
                def mq(e):
                    e.tensor_tensor(out=qi[:, oc, :], in0=qf[:, 0:T], in1=dd[:, oc, :], op=ALU.mult)
                    ins = None
                    for b in range(NSMP):
                        ins = e.tensor_copy(out=qm[:, b, oc, b:b + 1], in_=qf[:, T + b:T + b + 1])
                    return ins
                P.op("dve", mq, [R["qf"], R["dd"]], [R["qi"], R["qm"]])
            proj2(wq, hh * 256, slot(), cons_q, after_q)
            for vb in range(2):
                def cons_r(oc, ci, c0, c1, pap, Rp, vb=vb):
                    P.op("act", lambda e: e.activation(out=gr[:, vb * 2 + oc, c0:c1], in_=pap, func=AF.Silu), [Rp], [R["gr"]])
                proj2(wr, hh * 512 + vb * 256, slot(), cons_r, lambda oc: None)
            dma("sp", Sm[:].rearrange("p a v -> p (a v)"), cc_S_out[hh].ap()[0:128, :], [Rcc], R_Sm)
            P.op("dve", lambda e: e.tensor_scalar(out=Sm[:].rearrange("p a v -> p (a v)"), in0=Sm[:].rearrange("p a v -> p (a v)"),
                                                  scalar1=flag[:, 0:1], scalar2=None, op0=ALU.mult), R_Sm + [R_g], R_Sm)
            P.op("act", lambda e: e.copy(out=Sb[:], in_=Sm[:]), R_Sm, [R["Sb"]])

            PF = 3

            def load_tile(j, hh=hh):
                bj, ocj = j // 2, j % 2
                dma("sp", S0[j % NS0][:], st_S[bj, hh, ocj * 128:(ocj + 1) * 128, :], [], [R_S0[j % NS0]])

            def sample_tile(idx, hh=hh):
                b, oc = idx // 2, idx % 2
                si, sbi, vi = idx % NS0, idx % 3, b % 2
                pob = idx % 2
                Rsb, Rvm = R["S0b%d" % sbi], R["vm%d" % vi]
                if oc == 0:
                    P.op("dve", lambda e: e.tensor_scalar(out=vm[vi][:], in0=vst[:], scalar1=ident[0:NSMP, b:b + 1],
                                                          scalar2=None, op0=ALU.mult), [R["vst"], R_c], [Rvm])
                if idx == 0:
                    for j in range(PF):
                        load_tile(j)
                if idx + PF < 2 * NSMP:
                    load_tile(idx + PF)
                P.op("pe", lambda e: e.matmul(psb[pob][:, :], lhsT=kst[:, oc * 128:(oc + 1) * 128], rhs=vm[vi][:], start=True, stop=True),
                     [R["kst"], Rvm], [R_ps[pob]])
                P.op("dve", lambda e: e.scalar_tensor_tensor(
                    out=S0[si][:], in0=S0[si][:], scalar=als[:, hh * 2 + oc, b:b + 1], in1=psb[pob][:, :], op0=ALU.mult, op1=ALU.add),
                    [R_ps[pob], R_S0[si], R["als"]], [R_S0[si]])
                dma("sp", s_S[b, hh, oc * 128:(oc + 1) * 128, :], S0[si][:], [R_S0[si]], [R_out])
                P.op("act", lambda e: e.copy(out=S0b[sbi][:], in_=S0[si][:]), [R_S0[si]], [Rsb])
                P.op("pe", lambda e: e.matmul(psb[2][0:NSMP, :], lhsT=qm[:, b, oc, :], rhs=S0b[sbi][:],
                                              start=(oc == 0), stop=(oc == 1)), [R["qm"], Rsb], [R_ps[2]])
                if oc == 1:
                    if b == 0:
                        P.op("dve", lambda e: e.tensor_copy(out=osa[:], in_=psb[2][0:NSMP, :]), [R_ps[2]], [R["osa"]])
                    else:
                        P.op("dve", lambda e: e.tensor_tensor(out=osa[:], in0=psb[2][0:NSMP, :], in1=osa[:], op=ALU.add),
                             [R_ps[2], R["osa"]], [R["osa"]])

            tile_i = [0]

            def some_tiles(k):
                for _ in range(k):
                    if tile_i[0] < 2 * NSMP:
                        sample_tile(tile_i[0])
                        tile_i[0] += 1

            scmv, onv = (scm, scm2), (on, on2)
            for c in range(8):
                cs = slice(c * 128, (c + 1) * 128)
                par = c % 2
                Rsc, Rscm, Ron, Rsst = R["sc%d" % par], R["scm%d" % par], R["on%d" % par], R["sst%d" % par]
                scp = psb[6][:, 256 + par * 128: 256 + (par + 1) * 128] if False else None
                pso = psb[4 + par]

                def mms(e, cs=cs, par=par):
                    e.matmul(psb[3][:, 384:512] if par else psb[7][:, 384:512], lhsT=ki[:, 0, cs], rhs=qi[:, 0, cs], start=True, stop=False)
                    return e.matmul(psb[3][:, 384:512] if par else psb[7][:, 384:512], lhsT=ki[:, 1, cs], rhs=qi[:, 1, cs], start=False, stop=True)
                def mms6(e, cs=cs, par=par):
                    e.matmul(psb[6][:, par * 128:(par + 1) * 128], lhsT=ki[:, 0, cs], rhs=qi[:, 0, cs], start=True, stop=False)
                    return e.matmul(psb[6][:, par * 128:(par + 1) * 128], lhsT=ki[:, 1, cs], rhs=qi[:, 1, cs], start=False, stop=True)
                P.op("pe", mms6, [R["ki"], R["qi"]], [Rsc])
                P.op("dve", lambda e, par=par: e.tensor_tensor(out=scmv[par][:], in0=psb[6][:, par * 128:(par + 1) * 128], in1=maskT[:], op=ALU.mult),
                     [Rsc, R_c], [Rscm])
                some_tiles(2)

                def mmo(e, cs=cs, c=c, par=par, pso=pso):
                    e.matmul(pso[:, :], lhsT=qi[:, 0, cs], rhs=Sb[:, 0, :], start=True, stop=False)
                    e.matmul(pso[:, :], lhsT=qi[:, 1, cs], rhs=Sb[:, 1, :], start=False, stop=False)
                    return e.matmul(pso[:, :], lhsT=scmv[par][:], rhs=vt[:, c, :], start=False, stop=True)
                P.op("pe", mmo, [R["qi"], R["Sb"], Rscm, R["vt"]], [R_ps[4 + par]])
                for oc in range(2):
                    P.op("pe", lambda e, c=c, oc=oc: e.matmul(psb[SB_[oc]][:, :], lhsT=keT[:, c, oc * 128:(oc + 1) * 128], rhs=vt[:, c, :], start=True, stop=True),
                         [R["keT"], R["vt"]], [R_ps[SB_[oc]]])
                    P.op("dve", lambda e, c=c, oc=oc: e.scalar_tensor_tensor(
                        out=Sm[:, oc, :], in0=Sm[:, oc, :], scalar=dd[:, oc, c * 128 + 127:c * 128 + 128], in1=psb[SB_[oc]][:, :],
                        op0=ALU.mult, op1=ALU.add), [R_ps[SB_[oc]], R_Sm[oc], R["dd"]], [R_Sm[oc]])
                P.op("pool", lambda e, par=par: e.memset(sst[:, par:par + 1], 0.0), [], [Rsst])
                P.op("act", lambda e, par=par, pso=pso: e.activation(out=onv[par][:], in_=pso[:, :], func=AF.Square, accum_out=sst[:, par:par + 1]),
                     [R_ps[4 + par]], [Ron, Rsst])
                P.op("act", lambda e, par=par: e.activation(out=sst[:, par:par + 1], in_=sst[:, par:par + 1], func=AF.Sqrt, scale=1.0 / 512.0, bias=EPS),
                     [Rsst], [Rsst])
                P.op("act", lambda e: e.copy(out=Sb[:], in_=Sm[:]), R_Sm, [R["Sb"]])
                some_tiles(0)
                P.op("dve", lambda e, par=par: e.reciprocal(out=sst[:, par:par + 1], in_=sst[:, par:par + 1]), [Rsst], [Rsst])
                P.op("dve", lambda e, par=par, pso=pso: e.scalar_tensor_tensor(out=onv[par][:], in0=pso[:, :], scalar=sst[:, par:par + 1], in1=onb[:],
                                                                               op0=ALU.mult, op1=ALU.mult),
                     [R_ps[4 + par], Rsst, Ron, R["sm"]], [Ron])
                some_tiles(2)

                def tro(e, par=par):
                    ins = None
                    for j in range(2):
                        ins = e.transpose(psb[6][:, 256 + j * 128:256 + (j + 1) * 128], onv[par][:, j * 128:(j + 1) * 128], ident[:, :])
                    return ins
                for hv in range(2):
                    Rtr = R["sst2"]
                    P.op("pe", lambda e, par=par, hv=hv: (e.transpose(psb[6][:, 256:384], onv[par][:, (2 * hv) * 128:(2 * hv + 1) * 128], ident[:, :]),
                                                          e.transpose(psb[6][:, 384:512], onv[par][:, (2 * hv + 1) * 128:(2 * hv + 2) * 128], ident[:, :]))[1],
                         [Ron, R_c], [Rtr])
                    P.op("dve", lambda e, c=c, hv=hv: e.tensor_tensor(
                        out=og[:, 2 * hv:2 * hv + 2, c * 128:(c + 1) * 128], in0=psb[6][:, 256:512].rearrange("p (j t) -> p j t", t=128),
                        in1=gr[:, 2 * hv:2 * hv + 2, c * 128:(c + 1) * 128], op=ALU.mult), [Rtr, R["gr"]], [R["og"]])
                some_tiles(0)
            some_tiles(2 * NSMP)
            dma("sp", p_S[hh].rearrange("(a p) v -> p a v", p=128), Sm[:], R_Sm, [R_out])
            P.op("pool", lambda e: e.memset(sst[:, 2:3], 0.0), [], [R["sst0"]])
            P.op("act", lambda e: e.activation(out=on[0:NSMP, :], in_=osa[:], func=AF.Square, accum_out=sst[0:NSMP, 2:3]),
                 [R["osa"]], [R["on0"], R["sst0"]])
            P.op("act", lambda e: e.activation(out=sst[0:NSMP, 2:3], in_=sst[0:NSMP, 2:3], func=AF.Sqrt, scale=1.0 / 512.0, bias=EPS),
                 [R["sst0"]], [R["sst0"]])
            P.op("dve", lambda e: e.reciprocal(out=sst[0:NSMP, 2:3], in_=sst[0:NSMP, 2:3]), [R["sst0"]], [R["sst0"]])
            P.op("dve", lambda e: e.scalar_tensor_tensor(out=on[0:NSMP, :], in0=osa[:], scalar=sst[0:NSMP, 2:3],
                                                         in1=onb[0:NSMP, :], op0=ALU.mult, op1=ALU.mult),
                 [R["osa"], R["sst0"], R["on0"], R["sm"]], [R["on0"]])

            def tros(e):
                ins = None
                for j in range(4):
                    ins = e.transpose(psb[3][:, j * 128:j * 128 + NSMP], on[0:NSMP, j * 128:(j + 1) * 128], ident[0:NSMP, 0:NSMP])
                return ins
            P.op("pe", tros, [R["on0"], R_c], [R_ps[3]])
            P.op("dve", lambda e: e.tensor_tensor(
                out=og[:, :, T:T + NSMP], in0=psb[3][:, :].rearrange("p (j t) -> p j t", t=128)[:, :, 0:NSMP],
                in1=gr[:, :, T:T + NSMP], op=ALU.mult), [R_ps[3], R["gr"]], [R["og"]])
            for half in range(2):
                sl = slot()
                wload(wO[sl][:], R_wA[sl], wo[hh * 512:(hh + 1) * 512, half * 1024:(half + 1) * 1024].rearrange("(k p) n -> p k n", p=128))
                for jj in range(8):
                    j = half * 8 + jj
                    for ci, (c0, c1) in enumerate(CG):
                        pi = nextps()
                        n_ = c1 - c0

                        def mmw(e, sl=sl, jj=jj, c0=c0, c1=c1, pi=pi, n_=n_):
                            ins = None
                            for k in range(4):
                                ins = e.matmul(psb[pi][:, 0:n_], lhsT=wO[sl][:, k, jj * 128:(jj + 1) * 128], rhs=og[:, k, c0:c1],
                                               start=(k == 0), stop=(k == 3))
                            return ins
                        P.op("pe", mmw, [R_wA[sl], R["og"]], [R_ps[pi]])
                        P.op("dve", lambda e, j=j, c0=c0, c1=c1, pi=pi, n_=n_: e.tensor_tensor(
                            out=xT[:, j, c0:c1], in0=psb[pi][:, 0:n_], in1=xT[:, j, c0:c1], op=ALU.add), [R_ps[pi], R_x[j]], [R_x[j]])
        barrier()

    import os
    STG = os.environ.get("KSTAGE", "all")
    ffn(0, W["ffn1_w1"], W["ffn1_w3"], W["ffn1_w2"], 0)
    rg_layer()
    if STG == "all":
        ffn(0, W["ffn2_w1"], W["ffn2_w3"], W["ffn2_w2"], 2)
        ffn(1, W["ffn1_w1"], W["ffn1_w3"], W["ffn1_w2"], 3)
    if STG in ("all", "gla"):
        gla_layer()
    if STG == "all":
        ffn(1, W["ffn2_w1"], W["ffn2_w3"], W["ffn2_w2"], 5)
    barrier()
    def emit_out(ci):
        if ci < 2:
            for tt in range(4 * ci, 4 * ci + 4):
                st = tt % 2
                fm_to_tm(lambda k, tt=tt: xT[:, k, tt * 128:(tt + 1) * 128], 128, y_p[tt * 128:(tt + 1) * 128, :],
                         stage[st], R_stage[st], st, lambda q: [R_x[q * 4 + j] for j in range(4)])
        else:
            fm_to_tm(lambda k: xT[:, k, T:T + NSMP], NSMP, y_s, stage[0], R_stage[0], 0, lambda q: [R_x[q * 4 + j] for j in range(4)])

    norm(6, out_fp32=True, after_group=emit_out)
    P.op("sp", lambda e: None, R_outs, [])
    P.emit(nc, es)
    es.close()
    return nc, P


_CACHE = {}


def kernel(**inputs):
    if "nc" not in _CACHE:
        _CACHE["nc"] = build_nc()
    nc, _ = _CACHE["nc"]
    f32 = lambda a: np.ascontiguousarray(np.asarray(a, dtype=np.float32))
    x_prompt = f32(inputs["x_prompt"])
    x_sample = f32(inputs["x_sample"])
    sh = f32(inputs["state_rglru_h"])
    scv = f32(inputs["state_rglru_conv"])
    sS = f32(inputs["state_gla_S"])
    wts = {n: f32(inputs[n]) for n in WNAMES}
    in_maps = []
    for c in range(8):
        b, half = c // 2, c % 2
        m = dict(wts)
        m["xp"] = x_prompt[b, half * T:(half + 1) * T]
        xs = np.zeros((NS, D), np.float32)
        xs[:NSMP] = x_sample[c * NSMP:(c + 1) * NSMP, 0]
        if half == 1:
            xs[NSMP:] = x_prompt[b, T - 3:T]
        m["xsm"] = xs
        m["st_h"] = sh[0, c * NSMP:(c + 1) * NSMP]
        m["st_conv"] = scv[0, c * NSMP:(c + 1) * NSMP].reshape(NSMP * 3, D)
        m["st_S"] = sS[0, c * NSMP:(c + 1) * NSMP]
        m["flag"] = np.full((128, 1), float(half), np.float32)
        in_maps.append(m)
    res = run_bass_kernel_spmd(nc, in_maps, core_ids=list(range(8)))
    r = res.results
    y_prompt = np.stack([np.concatenate([r[2 * b]["y_p"], r[2 * b + 1]["y_p"]], 0) for b in range(4)], 0)
    y_sample = np.concatenate([r[c]["y_s"] for c in range(8)], 0)[:, None, :]
    p_h = np.stack([r[2 * b + 1]["p_h"][0] for b in range(4)], 0)[None]
    p_conv = np.stack([r[2 * b + 1]["p_conv"] for b in range(4)], 0)[None]
    p_S = np.stack([r[2 * b + 1]["p_S"] for b in range(4)], 0)[None]
    s_h = np.concatenate([r[c]["s_h"] for c in range(8)], 0)[None]
    s_conv = np.concatenate([r[c]["s_conv"] for c in range(8)], 0)[None]
    s_S = np.concatenate([r[c]["s_S"] for c in range(8)], 0)[None]
    return (y_prompt, y_sample, p_h, p_conv, p_S, s_h, s_conv, s_S)
```
